# Optimizing a Trainium2 kernel written in Bass

```python
import jax, jax.numpy as jnp
from jax import lax
import numpy as np

D_MODEL = 1024
BATCH = 8
SEQ = 2048
DEPTH = 4

CHUNK = 64
Q_BLOCK = 128
N_MIXERS = 2
N_MLA_LAYERS = (DEPTH + 1) // 2
N_FOX_LAYERS = DEPTH // 2
D_FF = 256 * ((8 * D_MODEL // 3 + 255) // 256)
PLE_DIM = 256
MLA_HEADS = 8
MLA_NOPE = 128
MLA_ROPE = 64
MLA_V = 128
MLA_Q_LORA = 256
MLA_KV_LORA = 128
ROPE_THETA = 10000.0
FOX_HEADS = 8
FOX_HEAD_DIM = 128
NORM_EPS = 1e-6
NEG_INF = -1e30

kernel_name = "hybrid_mla_fox_macaron_trunk"


def rms_norm(x, g):
    xf = x.astype(jnp.float32)
    y = xf * lax.rsqrt(jnp.mean(xf * xf, axis=-1, keepdims=True) + NORM_EPS)
    return (y * g.astype(jnp.float32)).astype(x.dtype)


def swiglu(x, w_in, w_out):
    gate, up = jnp.split(x @ w_in, 2, axis=-1)
    return (jax.nn.silu(gate) * up) @ w_out


def apply_rope(x, cos, sin):
    x1, x2 = jnp.split(x, 2, axis=-1)
    return jnp.concatenate([x1 * cos - x2 * sin, x2 * cos + x1 * sin], axis=-1)


def causal_block_attention(scores_fn, v, chunk_causal):
    seq = v.shape[1]
    outs = []
    for q0 in range(0, seq, Q_BLOCK):
        q1 = q0 + Q_BLOCK
        s = scores_fn(q0, q1)
        q_pos = jnp.arange(q0, q1)
        k_pos = jnp.arange(q1)
        if chunk_causal:
            allowed = (k_pos[None, :] // CHUNK) <= (q_pos[:, None] // CHUNK)
        else:
            allowed = k_pos[None, :] <= q_pos[:, None]
        s = jnp.where(allowed, s, NEG_INF)
        probs = jax.nn.softmax(s, axis=-1).astype(v.dtype)
        outs.append(jnp.einsum('bhqk,bkhd->bqhd', probs, v[:, :q1]))
    return jnp.concatenate(outs, axis=1)


def mla_mixer(u, cos, sin, w_down, q_norm, w_uq, kv_norm, w_ukv, w_o):
    b, s, _ = u.shape
    c = u @ w_down
    c_q, c_kv, k_rope = jnp.split(c, [MLA_Q_LORA, MLA_Q_LORA + MLA_KV_LORA], axis=-1)
    q = (rms_norm(c_q, q_norm) @ w_uq).reshape(b, s, MLA_HEADS, MLA_NOPE + MLA_ROPE)
    q_nope, q_rope = jnp.split(q, [MLA_NOPE], axis=-1)
    q_rope = apply_rope(q_rope, cos[:, :, None, :], sin[:, :, None, :])
    k_rope = apply_rope(k_rope, cos, sin)
    kv = (rms_norm(c_kv, kv_norm) @ w_ukv).reshape(b, s, MLA_HEADS, MLA_NOPE + MLA_V)
    k_nope, v = jnp.split(kv, [MLA_NOPE], axis=-1)
    scale = (MLA_NOPE + MLA_ROPE) ** -0.5

    def scores(q0, q1):
        sc = (jnp.einsum('bqhd,bkhd->bhqk', q_nope[:, q0:q1], k_nope[:, :q1])
              + jnp.einsum('bqhr,bkr->bhqk', q_rope[:, q0:q1], k_rope[:, :q1]))
        return sc.astype(jnp.float32) * scale

    o = causal_block_attention(scores, v, chunk_causal=True)
    return o.reshape(b, s, MLA_HEADS * MLA_V) @ w_o


def fox_mixer(u, w_in, b_f, w_o):
    b, s, _ = u.shape
    hd = FOX_HEADS * FOX_HEAD_DIM
    proj = u @ w_in
    q, k, v, f_logit = jnp.split(proj, [hd, 2 * hd, 3 * hd], axis=-1)
    q = q.reshape(b, s, FOX_HEADS, FOX_HEAD_DIM)
    k = k.reshape(b, s, FOX_HEADS, FOX_HEAD_DIM)
    v = v.reshape(b, s, FOX_HEADS, FOX_HEAD_DIM)
    log_f = jax.nn.log_sigmoid(f_logit.astype(jnp.float32) + b_f.astype(jnp.float32))
    cum = jnp.cumsum(log_f, axis=1).transpose(0, 2, 1)
    scale = FOX_HEAD_DIM ** -0.5

    def scores(q0, q1):
        sc = jnp.einsum('bqhd,bkhd->bhqk', q[:, q0:q1], k[:, :q1]).astype(jnp.float32) * scale
        return sc + cum[:, :, q0:q1, None] - cum[:, :, None, :q1]

    o = causal_block_attention(scores, v, chunk_causal=False)
    return o.reshape(b, s, hd) @ w_o


def setup_inputs(seed: int = 0) -> dict:
    key = jax.random.key(seed)
    ks = jax.random.split(key, 20)
    f32 = jnp.float32

    def dense(k, shape, fan_in):
        return jax.random.normal(k, shape, f32) * (fan_in ** -0.5)

    def gain(k, shape):
        return 1.0 + 0.05 * jax.random.normal(k, shape, f32)

    x = jax.random.normal(ks[0], (BATCH, SEQ, D_MODEL), f32)
    p = jax.random.normal(ks[1], (DEPTH, BATCH, SEQ, PLE_DIM), f32)
    offset = jax.random.randint(ks[2], (BATCH, 1), 0, 4096, dtype=jnp.int32)
    positions = (offset + jnp.arange(SEQ, dtype=jnp.int32)[None, :]).astype(jnp.int32)

    ffn_norm = gain(ks[3], (DEPTH, 2, D_MODEL))
    ffn_w_in = dense(ks[4], (DEPTH, 2, D_MODEL, 2 * D_FF), D_MODEL)
    ffn_w_out = dense(ks[5], (DEPTH, 2, D_FF, D_MODEL), D_FF)
    mix_norm = gain(ks[6], (DEPTH, D_MODEL))

    mla_w_down = dense(ks[7], (N_MLA_LAYERS, D_MODEL, MLA_Q_LORA + MLA_KV_LORA + MLA_ROPE), D_MODEL)
    mla_q_norm = gain(ks[8], (N_MLA_LAYERS, MLA_Q_LORA))
    mla_w_uq = dense(ks[9], (N_MLA_LAYERS, MLA_Q_LORA, MLA_HEADS * (MLA_NOPE + MLA_ROPE)), MLA_Q_LORA)
    mla_kv_norm = gain(ks[10], (N_MLA_LAYERS, MLA_KV_LORA))
    mla_w_ukv = dense(ks[11], (N_MLA_LAYERS, MLA_KV_LORA, MLA_HEADS * (MLA_NOPE + MLA_V)), MLA_KV_LORA)
    mla_w_o = dense(ks[12], (N_MLA_LAYERS, MLA_HEADS * MLA_V, D_MODEL), MLA_HEADS * MLA_V)

    fox_hd = FOX_HEADS * FOX_HEAD_DIM
    fox_w_in = dense(ks[13], (N_FOX_LAYERS, D_MODEL, 3 * fox_hd + FOX_HEADS), D_MODEL)
    fox_w_in = fox_w_in.at[..., 3 * fox_hd:].multiply(0.1)
    fox_b_f = jax.random.uniform(ks[14], (N_FOX_LAYERS, FOX_HEADS), f32, minval=1.0, maxval=6.0)
    fox_w_o = dense(ks[15], (N_FOX_LAYERS, fox_hd, D_MODEL), fox_hd)

    ple_norm = gain(ks[16], (DEPTH, D_MODEL))
    ple_w_gate = dense(ks[17], (DEPTH, D_MODEL, D_MODEL), D_MODEL)
    ple_w_proj = dense(ks[18], (DEPTH, PLE_DIM, D_MODEL), PLE_DIM)
    final_norm = gain(ks[19], (D_MODEL,))

    return {"x": x, "p": p, "positions": positions,
            "ffn_norm": ffn_norm, "ffn_w_in": ffn_w_in, "ffn_w_out": ffn_w_out,
            "mix_norm": mix_norm,
            "mla_w_down": mla_w_down, "mla_q_norm": mla_q_norm, "mla_w_uq": mla_w_uq,
            "mla_kv_norm": mla_kv_norm, "mla_w_ukv": mla_w_ukv, "mla_w_o": mla_w_o,
            "fox_w_in": fox_w_in, "fox_b_f": fox_b_f, "fox_w_o": fox_w_o,
            "ple_norm": ple_norm, "ple_w_gate": ple_w_gate, "ple_w_proj": ple_w_proj,
            "final_norm": final_norm}


def reference(x, p, positions, ffn_norm, ffn_w_in, ffn_w_out, mix_norm,
              mla_w_down, mla_q_norm, mla_w_uq, mla_kv_norm, mla_w_ukv, mla_w_o,
              fox_w_in, fox_b_f, fox_w_o, ple_norm, ple_w_gate, ple_w_proj, final_norm):
    inv_freq = ROPE_THETA ** (-jnp.arange(0, MLA_ROPE, 2, dtype=jnp.float32) / MLA_ROPE)
    ang = positions.astype(jnp.float32)[..., None] * inv_freq
    cos = jnp.cos(ang).astype(x.dtype)
    sin = jnp.sin(ang).astype(x.dtype)

    h = x
    for i in range(DEPTH):
        h = h + 0.5 * swiglu(rms_norm(h, ffn_norm[i, 0]), ffn_w_in[i, 0], ffn_w_out[i, 0])
        u = rms_norm(h, mix_norm[i])
        j = i // N_MIXERS
        if i % N_MIXERS == 0:
            h = h + mla_mixer(u, cos, sin, mla_w_down[j], mla_q_norm[j], mla_w_uq[j],
                              mla_kv_norm[j], mla_w_ukv[j], mla_w_o[j])
        else:
            h = h + fox_mixer(u, fox_w_in[j], fox_b_f[j], fox_w_o[j])
        h = h + 0.5 * swiglu(rms_norm(h, ffn_norm[i, 1]), ffn_w_in[i, 1], ffn_w_out[i, 1])
        gate = jax.nn.sigmoid(rms_norm(h, ple_norm[i]) @ ple_w_gate[i])
        h = h + gate * (p[i] @ ple_w_proj[i])
    return rms_norm(h, final_norm)
```

```python
import contextlib
import numpy as np
import concourse.bass as bass
import concourse.mybir as mybir

F32 = mybir.dt.float32
BF16 = mybir.dt.bfloat16
I32 = mybir.dt.int32
AF = mybir.ActivationFunctionType
ALU = mybir.AluOpType

ENGS = ("pe", "act", "dve", "pool", "sp")
SELF_SYNC = {"pe": False, "act": True, "dve": True, "pool": True, "sp": False}


class Res:
    __slots__ = ("name", "w", "r")

    def __init__(self, name, fence=None):
        self.name = name
        self.w = None
        self.r = dict(fence) if fence else {}


def fence_of(resources):
    f = {}
    for r in resources:
        if r.w is not None:
            k, v = r.w
            if f.get(k, 0) < v:
                f[k] = v
        for k, v in r.r.items():
            if f.get(k, 0) < v:
                f[k] = v
    return f


class Prog:
    def __init__(self, nc):
        self.nc = nc
        self.q = {e: [] for e in ENGS}
        self.cnt = {e: 0 for e in ENGS}
        self.know = {e: {} for e in ENGS}
        self.evknow = {}
        self.dcnt = {}
        self.sems = {}
        self.stack = contextlib.ExitStack()
        for e in ENGS:
            self.sems[e] = self.stack.enter_context(nc.semaphore("s_" + e))
        self.nwaits = 0
        self.nops = 0

    def dma_sem(self, name):
        key = "d_" + name
        self.sems[key] = self.stack.enter_context(self.nc.semaphore(key))
        self.dcnt[key] = 0
        return key

    def op(self, eng, fn, reads=(), writes=(), dsem=None, extra=()):
        waits = {}

        def need(ev):
            if ev is None:
                return
            k, v = ev
            if waits.get(k, 0) < v:
                waits[k] = v

        for r in reads:
            need(r.w)
        for w in writes:
            need(w.w)
            for k, v in w.r.items():
                need((k, v))
        for ev in extra:
            need(ev)
        kn = self.know[eng]
        final = []
        for k, v in waits.items():
            if k == eng and not SELF_SYNC[eng]:
                continue
            if kn.get(k, 0) >= v:
                continue
            final.append((k, v))
        for k, v in final:
            if kn.get(k, 0) < v:
                kn[k] = v
            snap = self.evknow.get((k, v))
            if snap:
                for kk, vv in snap.items():
                    if kn.get(kk, 0) < vv:
                        kn[kk] = vv
        if dsem is not None:
            self.dcnt[dsem] += 16
            ev = (dsem, self.dcnt[dsem])
            self.evknow[ev] = dict(kn)
            inc = (dsem, 16)
        else:
            self.cnt[eng] += 1
            ev = (eng, self.cnt[eng])
            snap = dict(kn)
            snap[eng] = self.cnt[eng]
            self.evknow[ev] = snap
            inc = (eng, 1)
        self.q[eng].append((final, fn, inc))
        self.nwaits += len(final)
        self.nops += 1
        for r in reads:
            if r.r.get(ev[0], 0) < ev[1]:
                r.r[ev[0]] = ev[1]
        for w in writes:
            w.w = ev
            w.r = {}
        return ev

    def wait_only(self, eng, events):
        kn = self.know[eng]
        mx = {}
        for k, v in events:
            if mx.get(k, 0) < v:
                mx[k] = v
        final = [(k, v) for (k, v) in mx.items() if kn.get(k, 0) < v]
        for k, v in final:
            kn[k] = v
        self.q[eng].append((final, None, None))

    def emit(self):
        nc = self.nc
        sems = self.sems
        q = self.q

        def run(e, lst):
            for waits, fn, inc in lst:
                for k, v in waits:
                    e.wait_ge(sems[k], v)
                if fn is None:
                    continue
                ins = fn(e)
                ins.then_inc(sems[inc[0]], inc[1])

        with nc.Block() as block:
            @block.tensor
            def _(e):
                run(e, q["pe"])

            @block.scalar
            def _(e):
                run(e, q["act"])

            @block.vector
            def _(e):
                run(e, q["dve"])

            @block.gpsimd
            def _(e):
                run(e, q["pool"])

            @block.sync
            def _(e):
                run(e, q["sp"])
        self.stack.close()

from concourse.bass_utils import run_bass_kernel_spmd

D = 1024
S = 2048
NT = 4
DFF = 2816
DEPTH = 4
EPS = 1e-6
MLA_SCALE = 192.0 ** -0.5
FOX_SCALE = 128.0 ** -0.5
NEG = -30000.0
PI = float(np.pi)

W_A, W_B, W_C, W_SLOT, W_K, W_E = 16384, 8192, 8192, 4096, 672, 7472
ARENA = W_A + W_B + W_C + 3 * W_SLOT + W_K + W_E
assert ARENA <= 53200, ARENA


def MM(out, lhsT, rhs, start=True, stop=True):
    return lambda e: e.matmul(out, lhsT=lhsT, rhs=rhs, start=start, stop=stop)


def TR(out, in_, ident):
    return lambda e: e.transpose(out, in_, ident)


def GROUP(lst):
    def f(e):
        ins = None
        for g in lst:
            ins = g(e)
        return ins
    return f


def ACTV(out, in_, func, **kw):
    return lambda e: e.activation(out=out, in_=in_, func=func, **kw)


def STT(out, in0, scalar, in1, op0, op1):
    return lambda e: e.scalar_tensor_tensor(out=out, in0=in0, scalar=scalar, in1=in1, op0=op0, op1=op1)


def TS(out, in0, s1, s2=None, op0=ALU.mult, op1=None):
    if op1 is None:
        return lambda e: e.tensor_scalar(out=out, in0=in0, scalar1=s1, scalar2=None, op0=op0)
    return lambda e: e.tensor_scalar(out=out, in0=in0, scalar1=s1, scalar2=s2, op0=op0, op1=op1)


def TT(out, in0, in1, op):
    return lambda e: e.tensor_tensor(out=out, in0=in0, in1=in1, op=op)


def CP(out, in_):
    return lambda e: e.tensor_copy(out=out, in_=in_)


def DMA(out, in_):
    return lambda e: e.dma_start(out=out, in_=in_)


class Region:
    def __init__(self, arena, base, words):
        self.arena, self.base, self.words = arena, base, words
        self.fence = {}
        self.live = []
        self.off = 0

    def reset(self):
        f = fence_of(self.live)
        for k, v in f.items():
            if self.fence.get(k, 0) < v:
                self.fence[k] = v
        self.live = []
        self.off = 0

    def res(self, name):
        r = Res(name, self.fence)
        self.live.append(r)
        return r

    def raw(self, words, at=None):
        if at is None:
            at = self.off
            self.off += words
        assert at + words <= self.words, (at, words, self.words)
        return self.arena[:, self.base + at:self.base + at + words]

    def f32(self, words, at=None):
        return self.raw(words, at)

    def bf(self, elems, at=None):
        assert elems % 2 == 0
        return self.raw(elems // 2, at).bitcast(BF16)

    def i32(self, words, at=None):
        return self.raw(words, at).bitcast(I32)


class Rot:
    def __init__(self, items):
        self.items = list(items)
        self.i = 0

    def next(self):
        x = self.items[self.i % len(self.items)]
        self.i += 1
        return x


def build(layers, first, last):
    nc = bass.Bass("TRN2", target_bir_lowering=False)
    dr = {}

    def din(name, shape, dt=F32):
        dr[name] = nc.dram_tensor(name, list(shape), dt, kind="ExternalInput").ap()
        return dr[name]

    if first:
        x_d = din("x", [S, D])
    else:
        hin_d = din("hin", [128, 8, S])
    p_d = din("p", [DEPTH, S, 256])
    pos_d = din("pos", [1, S], I32)
    w_in_d = din("ffn_w_in", [DEPTH, 2, D, 2 * DFF])
    w_out_d = din("ffn_w_out", [DEPTH, 2, DFF, D])
    wdown_d = din("mla_w_down", [2, D, 448])
    wuq_d = din("mla_w_uq", [2, 256, 1536])
    wukv_d = din("mla_w_ukv", [2, 128, 2048])
    wo_mla_d = din("mla_w_o", [2, D, D])
    fox_in_d = din("fox_w_in", [2, D, 3080])
    wo_fox_d = din("fox_w_o", [2, D, D])
    wgate_d = din("ple_w_gate", [DEPTH, D, D])
    wproj_d = din("ple_w_proj", [DEPTH, 256, D])
    gtab_d = din("gtab", [142, 128])
    bft_d = din("bft", [8, 2])
    ident_d = din("ident", [128, 128])
    maskT_d = din("maskT", [128, 128])
    maskC_d = din("maskC", [128, 128])
    invf_d = din("invf", [64, 1])
    if last:
        out_d = nc.dram_tensor("out", [S, D], F32, kind="ExternalOutput").ap()
    else:
        hout_d = nc.dram_tensor("hout", [128, 8, S], F32, kind="ExternalOutput").ap()

    st = contextlib.ExitStack()
    with st:
        arena = st.enter_context(nc.sbuf_tensor("arena", [128, ARENA], F32))
        psum = st.enter_context(nc.psum_tensor("ps", [128, 8, 512], F32))
        P = Prog(nc)
        st.enter_context(P.stack)
        with contextlib.suppress(Exception):
            st.enter_context(nc.allow_low_precision("bf16 operands by design"))

        o = 0
        hT = arena[:, o:o + W_A].rearrange("p (c n) -> p c n", c=8); o += W_A
        RB = Region(arena, o, W_B); o += W_B
        RC = Region(arena, o, W_C); o += W_C
        slot_base = o; o += 3 * W_SLOT
        RK = Region(arena, o, W_K); o += W_K
        RE = Region(arena, o, W_E); o += W_E
        assert o == ARENA

        hres = [[Res("h%d_%d" % (c, t)) for t in range(NT)] for c in range(8)]
        bank = [psum[:, b, :] for b in range(8)]
        bres = [Res("bank%d" % b) for b in range(8)]
        slots = [arena[:, slot_base + i * W_SLOT: slot_base + (i + 1) * W_SLOT].bitcast(BF16) for i in range(3)]
        sres = [Res("slot%d" % i) for i in range(3)]
        ssem = [P.dma_sem("slot%d" % i) for i in range(3)]
        srot = Rot(range(3))
        misc_sem = P.dma_sem("misc")
        out_sems = [P.dma_sem("out0"), P.dma_sem("out1")]
        out_events = []

        def tcols(t):
            return slice(t * 512, (t + 1) * 512)

        ident = RK.f32(128)
        ones_bf = RK.bf(128)
        gcol = RK.f32(142)
        maskT = RK.f32(128)
        maskC = RK.f32(128)
        invf = RK.f32(2)
        bfcol = RK.f32(2)
        negb = RK.f32(2)
        wf = RK.bf(64).rearrange("p (k n) -> p k n", k=8)
        kres = RK.res("consts")
        wfres = RK.res("wf")
        wf_sem = P.dma_sem("wf")
        sq = [RE.bf(512), RE.bf(512)]
        sqres = [Res("sq0"), Res("sq1")]
        E_PH = RE.off
        RE2 = Region(arena, RE.base + E_PH, W_E - E_PH)

        evs = []
        evs.append(P.op("sp", DMA(ident, ident_d), writes=[kres], dsem=misc_sem))
        P.op("sp", DMA(maskT, maskT_d), writes=[kres], dsem=misc_sem)
        P.op("sp", DMA(maskC, maskC_d), writes=[kres], dsem=misc_sem)
        P.op("sp", DMA(invf[0:64, 0:1], invf_d), writes=[kres], dsem=misc_sem)
        P.op("sp", DMA(bfcol[0:8, 0:2], bft_d), writes=[kres], dsem=misc_sem)
        g1 = RC.f32(128)
        g2 = RC.f32(128)
        gres = RC.res("gstage")
        gt_sem = P.dma_sem("gtab")
        P.op("sp", DMA(g1, gtab_d[0:128, :]), writes=[gres], dsem=gt_sem)
        P.op("sp", DMA(g2[0:14, :], gtab_d[128:142, :]), writes=[gres], dsem=gt_sem)
        P.op("dve", lambda e: e.memset(ones_bf, 1.0), writes=[kres])
        P.op("dve", TS(negb[0:8, 0:2], bfcol[0:8, 0:2], -1.0), reads=[kres], writes=[kres])
        P.op("pe", TR(bank[0][:, 0:128], g1, ident), reads=[gres, kres], writes=[bres[0]])
        P.op("pe", TR(bank[1][:, 0:14], g2[0:14, :], ident[0:14, 0:14]), reads=[gres, kres], writes=[bres[1]])
        P.op("dve", CP(gcol[:, 0:128], bank[0][:, 0:128]), reads=[bres[0]], writes=[kres])
        P.op("dve", CP(gcol[:, 128:142], bank[1][:, 0:14]), reads=[bres[1]], writes=[kres])
        RC.reset()

        if first:
            NX = 6
            xst = [RC.f32(1024) for _ in range(NX)]
            xres = [RC.res("xst%d" % i) for i in range(NX)]
            xsem = [P.dma_sem("xst%d" % i) for i in range(NX)]
            brot = Rot([0, 1, 2, 3])
            for tt in range(16):
                b = tt % NX
                P.op("sp", DMA(xst[b], x_d[tt * 128:(tt + 1) * 128, :]), writes=[xres[b]], dsem=xsem[b])
                for half in range(2):
                    bk = brot.next()
                    for cc in range(4):
                        c = half * 4 + cc
                        P.op("pe", TR(bank[bk][:, cc * 128:(cc + 1) * 128], xst[b][:, c * 128:(c + 1) * 128], ident),
                             reads=[xres[b], kres], writes=[bres[bk]])
                    eng = "act" if half == 0 else "dve"
                    dst = hT[:, half * 4:half * 4 + 4, tt * 128:(tt + 1) * 128]
                    src = bank[bk].rearrange("p (c n) -> p c n", c=4)
                    wr = [hres[half * 4 + cc][tt // 4] for cc in range(4)]
                    if eng == "act":
                        P.op("act", ACTV(dst, src, AF.Copy), reads=[bres[bk]], writes=wr)
                    else:
                        P.op("dve", CP(dst, src), reads=[bres[bk]], writes=wr)
            RC.reset()
        else:
            for c in range(8):
                hsem = P.dma_sem("hin%d" % c)
                P.op("sp", DMA(hT[:, c, :], hin_d[:, c, :]), writes=hres[c], dsem=hsem)

        nrot = Rot([6, 7])

        def new_u():
            RB.reset()
            u = RB.bf(8 * S).rearrange("p (c n) -> p c n", c=8)
            ures = [[RB.res("u%d_%d" % (c, t)) for t in range(NT)] for c in range(8)]
            return u, ures

        def rstd_bank(t, bk, srcs, src_res, nfeat):
            n = len(srcs)
            for c in range(n):
                q = c % 2
                P.op("act", ACTV(sq[q], srcs[c], AF.Square), reads=[src_res[c]], writes=[sqres[q]])
                P.op("pe", MM(bank[bk], ones_bf, sq[q], start=(c == 0), stop=(c == n - 1)),
                     reads=[sqres[q], kres], writes=[bres[bk]])
            P.op("act", ACTV(bank[bk], bank[bk], AF.Ln, scale=1.0 / nfeat, bias=EPS), reads=[bres[bk]], writes=[bres[bk]])

        def norm_to_u(gidx):
            u, ures = new_u()
            for t in range(NT):
                bk = nrot.next()
                rstd_bank(t, bk, [hT[:, c, tcols(t)] for c in range(8)], [hres[c][t] for c in range(8)], D)
                P.op("act", ACTV(bank[bk], bank[bk], AF.Exp, scale=-0.5), reads=[bres[bk]], writes=[bres[bk]])
                for c in range(8):
                    P.op("dve", STT(u[:, c, tcols(t)], hT[:, c, tcols(t)], gcol[:, gidx + c:gidx + c + 1], bank[bk], ALU.mult, ALU.mult),
                         reads=[hres[c][t], bres[bk], kres], writes=[ures[c][t]])
            return u, ures

        def get_slot():
            i = srot.next()
            return i, slots[i], sres[i], ssem[i]

        def wdma(sr, sem, out, in_):
            return P.op("pool", DMA(out, in_), writes=[sr], dsem=sem)

        def proj_to_h(u_like, ures_like, nk, wslot, wres, scale, brot_):
            for d in range(8):
                for t in range(NT):
                    bk = brot_.next()
                    P.op("pe", GROUP([MM(bank[bk], wslot[:, k, d * 128:(d + 1) * 128], u_like[:, k, tcols(t)], k == 0, k == nk - 1) for k in range(nk)]),
                         reads=[wres] + [ures_like[k][t] for k in range(nk)], writes=[bres[bk]])
                    P.op("dve", STT(hT[:, d, tcols(t)], bank[bk], scale, hT[:, d, tcols(t)], ALU.mult, ALU.add),
                         reads=[bres[bk], hres[d][t]], writes=[hres[d][t]])

        def ffn(l, j):
            u, ures = norm_to_u((l * 2 + j) * 8)
            RC.reset()
            RE2.reset()
            g = RC.bf(8 * S).rearrange("p (c n) -> p c n", c=8)
            gres_ = [[RC.res("g%d_%d" % (c, t)) for t in range(NT)] for c in range(8)]
            stmp = [RE2.f32(512), RE2.f32(512)]
            stres = [RE2.res("st0"), RE2.res("st1")]
            strot = Rot([0, 1])
            prot = Rot([(0, 1), (2, 3), (4, 5)])
            orot = Rot([6, 7])
            win = w_in_d[l, j].rearrange("(k p) n -> p k n", p=128)
            wout = w_out_d[l, j].rearrange("(k p) n -> p k n", p=128)
            f0 = 0
            for grp in ([4, 4], [4, 4], [4, 2]):
                gi = 0
                fk0 = f0
                for nb in grp:
                    si, sl, sr, sem = get_slot()
                    wg = sl[:, 0:8 * nb * 128].rearrange("p (k n) -> p k n", k=8)
                    wu = sl[:, 4096:4096 + 8 * nb * 128].rearrange("p (k n) -> p k n", k=8)
                    wdma(sr, sem, wg, win[:, :, f0 * 128:(f0 + nb) * 128])
                    wdma(sr, sem, wu, win[:, :, DFF + f0 * 128:DFF + (f0 + nb) * 128])
                    for f in range(nb):
                        for t in range(NT):
                            bg, bu = prot.next()
                            rd = [sr] + [ures[k][t] for k in range(8)]
                            P.op("pe", GROUP([MM(bank[bg], wg[:, k, f * 128:(f + 1) * 128], u[:, k, tcols(t)], k == 0, k == 7) for k in range(8)]),
                                 reads=rd, writes=[bres[bg]])
                            P.op("pe", GROUP([MM(bank[bu], wu[:, k, f * 128:(f + 1) * 128], u[:, k, tcols(t)], k == 0, k == 7) for k in range(8)]),
                                 reads=rd, writes=[bres[bu]])
                            q = strot.next()
                            P.op("act", ACTV(stmp[q], bank[bg], AF.Silu), reads=[bres[bg]], writes=[stres[q]])
                            P.op("dve", TT(g[:, gi, tcols(t)], stmp[q], bank[bu], ALU.mult),
                                 reads=[stres[q], bres[bu]], writes=[gres_[gi][t]])
                        gi += 1
                        f0 += 1
                nk = gi
                si, sl, sr, sem = get_slot()
                wo = sl[:, 0:nk * 1024].rearrange("p (k n) -> p k n", k=nk)
                wdma(sr, sem, wo, wout[:, fk0:fk0 + nk, :])
                proj_to_h(g, gres_, nk, wo, sr, 0.5, orot)

        def ple(l):
            u, ures = norm_to_u(96 + l * 8)
            RC.reset()
            RE2.reset()
            pT = RC.bf(2 * S).rearrange("p (c n) -> p c n", c=2)
            pTres = [[RC.res("pT%d_%d" % (c, t)) for t in range(NT)] for c in range(2)]
            pst = [RC.f32(1024).rearrange("p (a n) -> p a n", a=4) for _ in range(2)]
            pstres = [RC.res("pst0"), RC.res("pst1")]
            psem = [P.dma_sem("pst%d_%d" % (l, i)) for i in range(2)]
            sig = [RC.f32(512), RC.f32(512)]
            sigres = [RC.res("sig0"), RC.res("sig1")]
            t2 = [RC.f32(512), RC.f32(512)]
            t2res = [RC.res("t20"), RC.res("t21")]
            si, sl, srg, sem = get_slot()
            wg = sl.rearrange("p (k n) -> p k n", k=8)
            wdma(srg, sem, wg, wgate_d[l].rearrange("(k p) n -> p k n", p=128))
            si, sl2, srp, sem2 = get_slot()
            wp = sl2[:, 0:2048].rearrange("p (k n) -> p k n", k=2)
            wdma(srp, sem2, wp, wproj_d[l].rearrange("(k p) n -> p k n", p=128))
            brot_ = Rot([0, 1, 2, 3])
            for t in range(NT):
                b = t % 2
                P.op("sp", DMA(pst[b], p_d[l, t * 512:(t + 1) * 512, :].rearrange("(a p) n -> p a n", p=128)),
                     writes=[pstres[b]], dsem=psem[b])
                for kk in range(2):
                    bk = brot_.next()
                    for a in range(4):
                        P.op("pe", TR(bank[bk][:, a * 128:(a + 1) * 128], pst[b][:, a, kk * 128:(kk + 1) * 128], ident),
                             reads=[pstres[b], kres], writes=[bres[bk]])
                    P.op("act", ACTV(pT[:, kk, tcols(t)], bank[bk], AF.Copy), reads=[bres[bk]], writes=[pTres[kk][t]])
            grot = Rot([(0, 1), (2, 3), (4, 5)])
            r2 = Rot([0, 1])
            for d in range(8):
                for t in range(NT):
                    bg, bp = grot.next()
                    P.op("pe", GROUP([MM(bank[bg], wg[:, k, d * 128:(d + 1) * 128], u[:, k, tcols(t)], k == 0, k == 7) for k in range(8)]),
                         reads=[srg] + [ures[k][t] for k in range(8)], writes=[bres[bg]])
                    P.op("pe", GROUP([MM(bank[bp], wp[:, k, d * 128:(d + 1) * 128], pT[:, k, tcols(t)], k == 0, k == 1) for k in range(2)]),
                         reads=[srp, pTres[0][t], pTres[1][t]], writes=[bres[bp]])
                    q = r2.next()
                    P.op("act", ACTV(sig[q], bank[bg], AF.Sigmoid), reads=[bres[bg]], writes=[sigres[q]])
                    P.op("dve", TT(t2[q], sig[q], bank[bp], ALU.mult), reads=[sigres[q], bres[bp]], writes=[t2res[q]])
                    P.op("dve", TT(hT[:, d, tcols(t)], hT[:, d, tcols(t)], t2[q], ALU.add),
                         reads=[t2res[q], hres[d][t]], writes=[hres[d][t]])

        def attention_head(h, kind, qn, kn, V, qkv_res, oT, oTres, tmp, tmpres, PT, PTres, acc, accres, extra):
            srot_ = Rot([0, 1, 2, 3])
            trot = Rot(range(len(tmp)))
            prot_ = Rot(range(len(PT)))
            blocks = []
            for j in range(NT):
                for i in range(4 * (j + 1)):
                    blocks.append((j, i))
            state = {}
            scale = MLA_SCALE if kind == "mla" else FOX_SCALE
            mask = maskC if kind == "mla" else maskT

            def geom(j, i):
                off = i - 4 * j
                c0 = max(0, off) * 128
                return off, c0

            def emit_S(n):
                j, i = blocks[n]
                off, c0 = geom(j, i)
                bk = srot_.next()
                state[n] = {"sb": bk}
                qs = slice(j * 512 + c0, (j + 1) * 512)
                ks = slice(i * 128, (i + 1) * 128)
                if kind == "mla":
                    qr, kr = extra["qr"], extra["krope"]
                    grp = [MM(bank[bk][:, c0:512], kn[:, ks], qn[:, qs], True, False),
                           MM(bank[bk][:, c0:512], kr[:, ks], qr[:, qs], False, True)]
                else:
                    grp = [MM(bank[bk][:, c0:512], kn[:, ks], qn[:, qs], True, False),
                           MM(bank[bk][:, c0:512], extra["sel"], extra["cumref"][:, qs], False, True)]
                P.op("pe", GROUP(grp), reads=qkv_res + extra.get("sres", []), writes=[bres[bk]])

            def emit_E(n):
                j, i = blocks[n]
                off, c0 = geom(j, i)
                bk = state[n]["sb"]
                pi = prot_.next()
                state[n]["pt"] = pi
                kw = {}
                rd = []
                if kind == "fox":
                    kw = dict(bias=extra["negcumK"][:, i * 8 + h:i * 8 + h + 1])
                    rd = extra["ncres"]
                if off < 0:
                    P.op("act", ACTV(PT[pi], bank[bk], AF.Exp, scale=scale, **kw), reads=[bres[bk]] + rd, writes=[PTres[pi]])
                else:
                    ti = trot.next()
                    P.op("dve", STT(tmp[ti][:, 0:128], bank[bk][:, c0:c0 + 128], scale, mask, ALU.mult, ALU.add),
                         reads=[bres[bk], kres], writes=[tmpres[ti]])
                    P.op("act", ACTV(PT[pi][:, c0:c0 + 128], tmp[ti][:, 0:128], AF.Exp, **kw), reads=[tmpres[ti]] + rd, writes=[PTres[pi]])
                    if c0 + 128 < 512:
                        P.op("act", ACTV(PT[pi][:, c0 + 128:512], bank[bk][:, c0 + 128:512], AF.Exp, scale=scale, **kw),
                             reads=[bres[bk]] + rd, writes=[PTres[pi]])

            pending = []

            def emit_PV(n):
                j, i = blocks[n]
                off, c0 = geom(j, i)
                pi = state[n]["pt"]
                ob, sb_ = 4 + (j % 2), 6 + (j % 2)
                aj = j % 2
                last_i = 4 * (j + 1) - 1
                P.op("pe", MM(bank[ob][:, c0:512], V[:, i, :], PT[pi][:, c0:512], i == 0, i == last_i),
                     reads=[PTres[pi]] + qkv_res, writes=[bres[ob]])
                if i == 0:
                    P.op("dve", CP(acc[aj], PT[pi]), reads=[PTres[pi]], writes=[accres[aj]])
                else:
                    P.op("dve", TT(acc[aj][:, c0:512], acc[aj][:, c0:512], PT[pi][:, c0:512], ALU.add),
                         reads=[PTres[pi], accres[aj]], writes=[accres[aj]])
                if i == last_i:
                    pending.append((n + 2, j, ob, sb_))
                del state[n]

            def emit_final(j, ob, sb_):
                aj = j % 2
                ph, pl = prot_.next(), prot_.next()
                P.op("dve", CP(PT[ph], acc[aj]), reads=[accres[aj]], writes=[PTres[ph]])
                P.op("dve", TT(PT[pl], acc[aj], PT[ph], ALU.subtract), reads=[accres[aj], PTres[ph]], writes=[PTres[pl]])
                P.op("pe", GROUP([MM(bank[sb_], ones_bf, PT[ph], True, False), MM(bank[sb_], ones_bf, PT[pl], False, True)]),
                     reads=[PTres[ph], PTres[pl], kres], writes=[bres[sb_]])
                P.op("act", ACTV(acc[aj], bank[sb_], AF.Ln), reads=[bres[sb_]], writes=[accres[aj]])
                P.op("act", ACTV(acc[aj], acc[aj], AF.Exp, scale=-1.0), reads=[accres[aj]], writes=[accres[aj]])
                P.op("dve", TT(oT[:, h, tcols(j)], bank[ob], acc[aj], ALU.mult),
                     reads=[bres[ob], accres[aj]], writes=[oTres[h][j]])

            nb = len(blocks)
            LA = 3
            for n in range(min(LA, nb)):
                emit_S(n)
            emit_E(0)
            for n in range(nb):
                if n + LA < nb:
                    emit_S(n + LA)
                if n + 1 < nb:
                    emit_E(n + 1)
                emit_PV(n)
                while pending and pending[0][0] <= n:
                    _, j_, ob_, sb__ = pending.pop(0)
                    emit_final(j_, ob_, sb__)
            while pending:
                _, j_, ob_, sb__ = pending.pop(0)
                emit_final(j_, ob_, sb__)

        def mla(l, jm):
            u, ures = norm_to_u(64 + l * 8)
            RC.reset()
            RE2.reset()
            cosT = RE2.bf(S)
            sinT = RE2.bf(S)
            cqn = RE2.bf(2 * S).rearrange("p (c n) -> p c n", c=2)
            ckvn = RE2.bf(S)
            krope = RE2.bf(S)
            tabres = RE2.res("tab")
            cqres = [[RE2.res("cq%d_%d" % (c, t)) for t in range(NT)] for c in range(2)]
            ckres = [RE2.res("ck%d" % t) for t in range(NT)]
            krres = [RE2.res("kr%d" % t) for t in range(NT)]
            P.op("dve", lambda e: e.memset(krope[64:128, :], 0.0), writes=krres)
            posi = RC.i32(2048)
            ang = RC.f32(2048)
            ki = RC.i32(2048)
            kf = RC.f32(2048)
            tr = RC.res("trig")
            psem_ = P.dma_sem("pos%d" % l)
            P.op("sp", DMA(posi[0:64, :], pos_d.partition_broadcast(64)), writes=[tr], dsem=psem_)
            C1 = 6.28125
            C2 = float(np.float32(2 * np.pi - 6.28125))
            A64 = slice(0, 64)
            P.op("dve", CP(ang[A64, :], posi[A64, :]), reads=[tr], writes=[tr])
            P.op("dve", TS(ang[A64, :], ang[A64, :], invf[0:64, 0:1]), reads=[tr, kres], writes=[tr])
            P.op("dve", TS(ki[A64, :], ang[A64, :], float(1.0 / (2 * np.pi))), reads=[tr], writes=[tr])
            P.op("dve", CP(kf[A64, :], ki[A64, :]), reads=[tr], writes=[tr])
            P.op("dve", STT(ang[A64, :], kf[A64, :], -C1, ang[A64, :], ALU.mult, ALU.add), reads=[tr], writes=[tr])
            P.op("dve", STT(ang[A64, :], kf[A64, :], -C2, ang[A64, :], ALU.mult, ALU.add), reads=[tr], writes=[tr])
            P.op("dve", TS(ang[A64, :], ang[A64, :], -3.1415925, 3.1415925, ALU.max, ALU.min), reads=[tr], writes=[tr])
            P.op("act", ACTV(sinT[A64, :], ang[A64, :], AF.Sin), reads=[tr], writes=[tabres])
            P.op("dve", STT(kf[A64, :], ang[A64, :], -1.0, ang[A64, :], ALU.mult, ALU.max), reads=[tr], writes=[tr])
            P.op("dve", TS(kf[A64, :], kf[A64, :], -1.0, PI / 2, ALU.mult, ALU.add), reads=[tr], writes=[tr])
            P.op("act", ACTV(cosT[A64, :], kf[A64, :], AF.Sin), reads=[tr], writes=[tabres])
            RC.reset()
            ia, sa, sra, sema = get_slot()
            wd = sa.rearrange("p (k n) -> p k n", k=8)
            wdma(sra, sema, wd[:, :, 0:448], wdown_d[jm].rearrange("(k p) n -> p k n", p=128))
            P.op("dve", TS(wd[:, :, 448:480], wd[:, :, 416:448], -1.0), reads=[sra], writes=[sra])
            P.op("dve", CP(wd[:, :, 480:512], wd[:, :, 384:416]), reads=[sra], writes=[sra])
            ib, sb, srb, semb = get_slot()
            wuq = sb[:, 0:3072].rearrange("p (k n) -> p k n", k=2)
            wukv = sb[:, 3072:5120]
            wuqr = sb[:, 5120:6144].rearrange("p (k n) -> p k n", k=2)
            wdma(srb, semb, wuq, wuq_d[jm].rearrange("(k p) n -> p k n", p=128))
            wdma(srb, semb, wukv, wukv_d[jm])
            wuq4 = wuq.rearrange("p k (h m) -> p k h m", h=8)
            wuqr4 = wuqr.rearrange("p k (h m) -> p k h m", h=8)
            P.op("dve", TS(wuqr4[:, :, :, 0:32], wuq4[:, :, :, 160:192], -1.0), reads=[srb], writes=[srb])
            P.op("dve", CP(wuqr4[:, :, :, 32:64], wuq4[:, :, :, 128:160]), reads=[srb], writes=[srb])
            sqd = [RC.bf(512) for _ in range(3)]
            sqdres = [RC.res("sqd%d" % i) for i in range(3)]
            rsb = [RC.f32(512), RC.f32(512)]
            rsres = [RC.res("rs0"), RC.res("rs1")]
            rt = [RC.f32(512), RC.f32(512)]
            rtres = [RC.res("rt0"), RC.res("rt1")]
            gq = 136 + jm * 2
            gkv = 140 + jm
            for t in range(NT):
                rd = [sra] + [ures[k][t] for k in range(8)]
                for bi, (c0_, c1_) in enumerate([(0, 128), (128, 256), (256, 384), (384, 448), (448, 512)]):
                    m = c1_ - c0_
                    P.op("pe", GROUP([MM(bank[bi][0:m, :], wd[:, k, c0_:c1_], u[:, k, tcols(t)], k == 0, k == 7) for k in range(8)]),
                         reads=rd, writes=[bres[bi]])
                for bi in range(3):
                    P.op("act", ACTV(sqd[bi], bank[bi], AF.Square), reads=[bres[bi]], writes=[sqdres[bi]])
                P.op("pe", MM(bank[5], ones_bf, sqd[0], True, False), reads=[sqdres[0], kres], writes=[bres[5]])
                P.op("pe", MM(bank[5], ones_bf, sqd[1], False, True), reads=[sqdres[1], kres], writes=[bres[5]])
                P.op("pe", MM(bank[6], ones_bf, sqd[2], True, True), reads=[sqdres[2], kres], writes=[bres[6]])
                P.op("act", ACTV(bank[5], bank[5], AF.Ln, scale=1.0 / 256, bias=EPS), reads=[bres[5]], writes=[bres[5]])
                P.op("act", ACTV(rsb[0], bank[5], AF.Exp, scale=-0.5), reads=[bres[5]], writes=[rsres[0]])
                P.op("act", ACTV(bank[6], bank[6], AF.Ln, scale=1.0 / 128, bias=EPS), reads=[bres[6]], writes=[bres[6]])
                P.op("act", ACTV(rsb[1], bank[6], AF.Exp, scale=-0.5), reads=[bres[6]], writes=[rsres[1]])
                for i in range(2):
                    P.op("dve", STT(cqn[:, i, tcols(t)], bank[i], gcol[:, gq + i:gq + i + 1], rsb[0], ALU.mult, ALU.mult),
                         reads=[bres[i], rsres[0], kres], writes=[cqres[i][t]])
                P.op("dve", STT(ckvn[:, tcols(t)], bank[2], gcol[:, gkv:gkv + 1], rsb[1], ALU.mult, ALU.mult),
                     reads=[bres[2], rsres[1], kres], writes=[ckres[t]])
                P.op("dve", TT(rt[0][A64, :], bank[3][A64, :], cosT[A64, tcols(t)], ALU.mult), reads=[bres[3], tabres], writes=[rtres[0]])
                P.op("dve", TT(rt[1][A64, :], bank[4][A64, :], sinT[A64, tcols(t)], ALU.mult), reads=[bres[4], tabres], writes=[rtres[1]])
                P.op("dve", TT(krope[A64, tcols(t)], rt[0][A64, :], rt[1][A64, :], ALU.add), reads=[rtres[0], rtres[1]], writes=[krres[t]])
            RC.reset()
            RB.reset()
            oT = RC.bf(8 * S).rearrange("p (c n) -> p c n", c=8)
            oTres = [[RC.res("o%d_%d" % (c, t)) for t in range(NT)] for c in range(8)]
            qn = RB.bf(S)
            qr = RB.bf(S)
            kn = RB.bf(S)
            V = RB.bf(S).rearrange("p (a n) -> p a n", a=16)
            hres_ = RB.res("headbufs")
            rt = [RB.f32(512), RB.f32(512)]
            rtres = [RB.res("rt0"), RB.res("rt1")]
            tmp = [RB.f32(128) for _ in range(3)]
            tmpres = [RB.res("tmp%d" % i) for i in range(3)]
            PT = [RB.bf(512) for _ in range(5)]
            PTres = [RB.res("PT%d" % i) for i in range(5)]
            acc = [RB.f32(512) for _ in range(2)]
            accres = [RB.res("acc%d" % i) for i in range(2)]
            prj = Rot([0, 1, 2, 3])
            P.op("dve", lambda e: e.memset(qr[64:128, :], 0.0), writes=[hres_])
            for h in range(8):
                for t in range(NT):
                    bk = prj.next()
                    P.op("pe", GROUP([MM(bank[bk], wuq[:, i, h * 192:h * 192 + 128], cqn[:, i, tcols(t)], i == 0, i == 1) for i in range(2)]),
                         reads=[srb, cqres[0][t], cqres[1][t]], writes=[bres[bk]])
                    P.op("act", ACTV(qn[:, tcols(t)], bank[bk], AF.Copy), reads=[bres[bk]], writes=[hres_])
                    ba = prj.next()
                    P.op("pe", GROUP([MM(bank[ba][A64, :], wuq[:, i, h * 192 + 128:h * 192 + 192], cqn[:, i, tcols(t)], i == 0, i == 1) for i in range(2)]),
                         reads=[srb, cqres[0][t], cqres[1][t]], writes=[bres[ba]])
                    bb = prj.next()
                    P.op("pe", GROUP([MM(bank[bb][A64, :], wuqr[:, i, h * 64:(h + 1) * 64], cqn[:, i, tcols(t)], i == 0, i == 1) for i in range(2)]),
                         reads=[srb, cqres[0][t], cqres[1][t]], writes=[bres[bb]])
                    P.op("dve", TT(rt[0][A64, :], bank[ba][A64, :], cosT[A64, tcols(t)], ALU.mult), reads=[bres[ba], tabres], writes=[rtres[0]])
                    P.op("dve", TT(rt[1][A64, :], bank[bb][A64, :], sinT[A64, tcols(t)], ALU.mult), reads=[bres[bb], tabres], writes=[rtres[1]])
                    P.op("dve", TT(qr[A64, tcols(t)], rt[0][A64, :], rt[1][A64, :], ALU.add), reads=[rtres[0], rtres[1]], writes=[hres_])
                    bk = prj.next()
                    P.op("pe", MM(bank[bk], wukv[:, h * 256:h * 256 + 128], ckvn[:, tcols(t)], True, True),
                         reads=[srb, ckres[t]], writes=[bres[bk]])
                    P.op("act", ACTV(kn[:, tcols(t)], bank[bk], AF.Copy), reads=[bres[bk]], writes=[hres_])
                    bk = prj.next()
                    P.op("pe", GROUP([MM(bank[bk][:, a * 128:(a + 1) * 128], ckvn[:, t * 512 + a * 128:t * 512 + (a + 1) * 128],
                                         wukv[:, h * 256 + 128:h * 256 + 256], True, True) for a in range(4)]),
                         reads=[srb, ckres[t]], writes=[bres[bk]])
                    P.op("act", ACTV(V[:, t * 4:(t + 1) * 4, :], bank[bk].rearrange("p (a n) -> p a n", a=4), AF.Copy),
                         reads=[bres[bk]], writes=[hres_])
                attention_head(h, "mla", qn, kn, V, [hres_] + krres, oT, oTres, tmp, tmpres, PT, PTres, acc, accres,
                               {"qr": qr, "krope": krope})
            ic, sc, src_, semc = get_slot()
            wo = sc.rearrange("p (k n) -> p k n", k=8)
            wdma(src_, semc, wo, wo_mla_d[jm].rearrange("(k p) n -> p k n", p=128))
            proj_to_h(oT, oTres, 8, wo, src_, 1.0, Rot([0, 1, 2, 3]))

        def fox(l, jf):
            u, ures = norm_to_u(64 + l * 8)
            RC.reset()
            RE2.reset()
            oT = RC.bf(8 * S).rearrange("p (c n) -> p c n", c=8)
            oTres = [[RC.res("o%d_%d" % (c, t)) for t in range(NT)] for c in range(8)]
            spb = RE2.f32(2048, at=4096)
            cumneg = RE2.f32(2048, at=0)
            fr1 = RE2.res("spb")
            fr = RE2.res("cumneg")
            win = fox_in_d[jf].rearrange("(k p) n -> p k n", p=128)
            P.op("pool", DMA(wf, win[:, :, 3072:3080]), writes=[wfres], dsem=wf_sem)
            A8 = slice(0, 8)
            prj = Rot([0, 1, 2, 3])
            for t in range(NT):
                bk = prj.next()
                P.op("pe", GROUP([MM(bank[bk][A8, :], wf[:, k, :], u[:, k, tcols(t)], k == 0, k == 7) for k in range(8)]),
                     reads=[wfres] + [ures[k][t] for k in range(8)], writes=[bres[bk]])
                P.op("act", ACTV(spb[A8, tcols(t)], bank[bk][A8, :], AF.Exp, scale=-1.0, bias=negb[0:8, jf:jf + 1]),
                     reads=[bres[bk], kres], writes=[fr1])
            P.op("act", ACTV(spb[A8, :], spb[A8, :], AF.Ln, bias=1.0), reads=[fr1], writes=[fr1])
            P.op("dve", lambda e: e.tensor_tensor_scan(out=cumneg[A8, :], data0=spb[A8, :], data1=spb[A8, :], initial=0.0, op0=ALU.add, op1=ALU.max),
                 reads=[fr1], writes=[fr])
            for k_, v_ in fence_of([fr1]).items():
                if RE2.fence.get(k_, 0) < v_:
                    RE2.fence[k_] = v_
            RE2.off = 3072
            cumref = RE2.bf(S)
            negcumK = RE2.f32(128)
            sel = RE2.bf(128)
            ncres = RE2.res("negcum")
            selres = RE2.res("sel")
            tmp = [RE2.f32(128) for _ in range(3)]
            tmpres = [RE2.res("tmp%d" % i) for i in range(3)]
            PT = [RE2.bf(512) for _ in range(4)]
            PTres = [RE2.res("PT%d" % i) for i in range(4)]
            acc = [RE2.f32(512) for _ in range(2)]
            accres = [RE2.res("acc%d" % i) for i in range(2)]
            bk = prj.next()
            P.op("pe", GROUP([TR(bank[bk][:, a * 8:(a + 1) * 8], cumneg[A8, a * 128:(a + 1) * 128], ident[0:8, 0:8]) for a in range(16)]),
                 reads=[fr, kres], writes=[bres[bk]])
            P.op("dve", CP(negcumK, bank[bk][:, 0:128]), reads=[bres[bk]], writes=[ncres])
            P.op("dve", lambda e: e.memset(cumref, 0.0), writes=[ncres])
            P.op("dve", TS(cumref[A8, :], cumneg[A8, :], -1.0 / FOX_SCALE), reads=[fr, ncres], writes=[ncres])
            hfence = fence_of([fr])
            qn = RE2.bf(S, at=0)
            kn = RE2.bf(S, at=1024)
            V = RE2.bf(S, at=2048).rearrange("p (a n) -> p a n", a=16)
            hres_ = Res("headbufs", {**RE2.fence, **{k: max(v, RE2.fence.get(k, 0)) for k, v in hfence.items()}})
            RE2.live.append(hres_)
            for h in range(8):
                ih, sh, srh, semh = get_slot()
                wh = sh[:, 0:8 * 384].rearrange("p (k n) -> p k n", k=8)
                for part in range(3):
                    wdma(srh, semh, wh[:, :, part * 128:(part + 1) * 128], win[:, :, part * 1024 + h * 128:part * 1024 + (h + 1) * 128])
                for t in range(NT):
                    rd = [srh] + [ures[k][t] for k in range(8)]
                    bk = prj.next()
                    P.op("pe", GROUP([MM(bank[bk], wh[:, k, 0:128], u[:, k, tcols(t)], k == 0, k == 7) for k in range(8)]), reads=rd, writes=[bres[bk]])
                    P.op("act", ACTV(qn[:, tcols(t)], bank[bk], AF.Copy), reads=[bres[bk]], writes=[hres_])
                    bk = prj.next()
                    P.op("pe", GROUP([MM(bank[bk], wh[:, k, 128:256], u[:, k, tcols(t)], k == 0, k == 7) for k in range(8)]), reads=rd, writes=[bres[bk]])
                    P.op("act", ACTV(kn[:, tcols(t)], bank[bk], AF.Copy), reads=[bres[bk]], writes=[hres_])
                    bk = prj.next()
                    grp = []
                    for a in range(4):
                        for k in range(8):
                            grp.append(MM(bank[bk][:, a * 128:(a + 1) * 128], u[:, k, t * 512 + a * 128:t * 512 + (a + 1) * 128], wh[:, k, 256:384], k == 0, k == 7))
                    P.op("pe", GROUP(grp), reads=rd, writes=[bres[bk]])
                    P.op("dve", CP(V[:, t * 4:(t + 1) * 4, :], bank[bk].rearrange("p (a n) -> p a n", a=4)), reads=[bres[bk]], writes=[hres_])
                P.op("dve", TS(sel, ones_bf, ident[:, h:h + 1]), reads=[kres], writes=[selres])
                attention_head(h, "fox", qn, kn, V, [hres_], oT, oTres, tmp, tmpres, PT, PTres, acc, accres,
                               {"sel": sel, "cumref": cumref, "negcumK": negcumK, "ncres": [ncres], "sres": [selres, ncres]})
            ic, sc, src_, semc = get_slot()
            wo = sc.rearrange("p (k n) -> p k n", k=8)
            wdma(src_, semc, wo, wo_fox_d[jf].rearrange("(k p) n -> p k n", p=128))
            proj_to_h(oT, oTres, 8, wo, src_, 1.0, Rot([0, 1, 2, 3]))

        for l in layers:
            ffn(l, 0)
            if l % 2 == 0:
                mla(l, l // 2)
            else:
                fox(l, l // 2)
            ffn(l, 1)
            ple(l)

        if last:
            RC.reset()
            RB.reset()
            yT = [RC.f32(512) for _ in range(8)]
            yres = [RC.res("y%d" % c) for c in range(8)]
            yst = [RC.f32(1024), RC.f32(1024)]
            ysres = [RC.res("ys0"), RC.res("ys1")]
            brot_ = Rot([0, 1, 2, 3])
            n_out = 0
            for t in range(NT):
                bk = nrot.next()
                rstd_bank(t, bk, [hT[:, c, tcols(t)] for c in range(8)], [hres[c][t] for c in range(8)], D)
                P.op("act", ACTV(bank[bk], bank[bk], AF.Exp, scale=-0.5), reads=[bres[bk]], writes=[bres[bk]])
                for c in range(8):
                    P.op("dve", STT(yT[c], hT[:, c, tcols(t)], gcol[:, 128 + c:129 + c], bank[bk], ALU.mult, ALU.mult),
                         reads=[hres[c][t], bres[bk], kres], writes=[yres[c]])
                for a in range(4):
                    b = n_out % 2
                    n_out += 1
                    for half in range(2):
                        tb = brot_.next()
                        for cc in range(4):
                            c = half * 4 + cc
                            P.op("pe", TR(bank[tb][:, cc * 128:(cc + 1) * 128], yT[c][:, a * 128:(a + 1) * 128], ident),
                                 reads=[yres[c], kres], writes=[bres[tb]])
                        if half == 0:
                            P.op("act", ACTV(yst[b][:, 0:512], bank[tb], AF.Copy), reads=[bres[tb]], writes=[ysres[b]])
                        else:
                            P.op("dve", CP(yst[b][:, 512:1024], bank[tb]), reads=[bres[tb]], writes=[ysres[b]])
                    r0 = t * 512 + a * 128
                    out_events.append(P.op("sp", DMA(out_d[r0:r0 + 128, :], yst[b]), reads=[ysres[b]], dsem=out_sems[b]))
        else:
            for c in range(8):
                out_events.append(P.op("sp", DMA(hout_d[:, c, :], hT[:, c, :]), reads=hres[c], dsem=out_sems[0]))
        P.wait_only("sp", out_events[-2:])
        P.emit()
    return nc, P


_CACHE = {}


def _get_prog(layers, first, last):
    key = (tuple(layers), first, last)
    if key not in _CACHE:
        _CACHE[key] = build(list(layers), first, last)[0]
    return _CACHE[key]


LAUNCH_PLAN = [([0, 1, 2, 3], True, True)]


def kernel(x, p, positions, ffn_norm, ffn_w_in, ffn_w_out, mix_norm,
           mla_w_down, mla_q_norm, mla_w_uq, mla_kv_norm, mla_w_ukv, mla_w_o,
           fox_w_in, fox_b_f, fox_w_o, ple_norm, ple_w_gate, ple_w_proj, final_norm):
    f32 = np.float32
    A = lambda a: np.ascontiguousarray(np.asarray(a))
    x = A(x); p = A(p); positions = A(positions)
    B = x.shape[0]
    gtab = np.concatenate([
        A(ffn_norm).reshape(-1, 128), A(mix_norm).reshape(-1, 128), A(ple_norm).reshape(-1, 128),
        A(final_norm).reshape(-1, 128), A(mla_q_norm).reshape(-1, 128), A(mla_kv_norm).reshape(-1, 128)], axis=0).astype(f32)
    assert gtab.shape == (142, 128)
    bft = A(A(fox_b_f).T).astype(f32)
    kk = np.arange(128)
    ident = np.eye(128, dtype=f32)
    maskT = np.where(kk[:, None] <= kk[None, :], 0.0, NEG).astype(f32)
    maskC = np.where((kk[:, None] // 64) <= (kk[None, :] // 64), 0.0, NEG).astype(f32)
    inv = (np.float32(10000.0) ** (-np.arange(0, 64, 2, dtype=f32) / np.float32(64))).astype(f32)
    invf = np.concatenate([inv, inv])[:, None].astype(f32)
    shared = {
        "ffn_w_in": A(ffn_w_in), "ffn_w_out": A(ffn_w_out), "mla_w_down": A(mla_w_down), "mla_w_uq": A(mla_w_uq),
        "mla_w_ukv": A(mla_w_ukv), "mla_w_o": A(mla_w_o), "fox_w_in": A(fox_w_in), "fox_w_o": A(fox_w_o),
        "ple_w_gate": A(ple_w_gate), "ple_w_proj": A(ple_w_proj), "gtab": gtab, "bft": bft, "ident": ident,
        "maskT": maskT, "maskC": maskC, "invf": invf,
    }
    h = None
    outp = None
    for layers, first, last in LAUNCH_PLAN:
        nc = _get_prog(layers, first, last)
        in_maps = []
        for b in range(B):
            m = dict(shared)
            m["p"] = A(p[:, b])
            m["pos"] = A(positions[b:b + 1]).astype(np.int32)
            if first:
                m["x"] = x[b]
            else:
                m["hin"] = h[b]
            in_maps.append(m)
        res = run_bass_kernel_spmd(nc, in_maps, core_ids=list(range(B)))
        if last:
            outp = np.stack([np.asarray(r["out"]) for r in res.results], axis=0).astype(f32)
        else:
            h = [np.asarray(r["hout"]) for r in res.results]
    return outp
```

```python
import contextlib
import numpy as np
import concourse.bass as bass
import concourse.mybir as mybir

F32 = mybir.dt.float32
BF16 = mybir.dt.bfloat16
I32 = mybir.dt.int32
AF = mybir.ActivationFunctionType
ALU = mybir.AluOpType

ENGS = ("pe", "act", "dve", "pool", "sp")
SELF_SYNC = {"pe": False, "act": True, "dve": True, "pool": True, "sp": False}


class Res:
    __slots__ = ("name", "w", "r")

    def __init__(self, name, fence=None):
        self.name = name
        self.w = None
        self.r = dict(fence) if fence else {}


def fence_of(resources):
    f = {}
    for r in resources:
        if r.w is not None:
            k, v = r.w
            if f.get(k, 0) < v:
                f[k] = v
        for k, v in r.r.items():
            if f.get(k, 0) < v:
                f[k] = v
    return f


class Prog:
    def __init__(self, nc):
        self.nc = nc
        self.q = {e: [] for e in ENGS}
        self.cnt = {e: 0 for e in ENGS}
        self.know = {e: {} for e in ENGS}
        self.evknow = {}
        self.dcnt = {}
        self.sems = {}
        self.stack = contextlib.ExitStack()
        for e in ENGS:
            self.sems[e] = self.stack.enter_context(nc.semaphore("s_" + e))
        self.nwaits = 0
        self.nops = 0

    def dma_sem(self, name):
        key = "d_" + name
        self.sems[key] = self.stack.enter_context(self.nc.semaphore(key))
        self.dcnt[key] = 0
        return key

    def op(self, eng, fn, reads=(), writes=(), dsem=None, extra=()):
        waits = {}

        def need(ev):
            if ev is None:
                return
            k, v = ev
            if waits.get(k, 0) < v:
                waits[k] = v

        for r in reads:
            need(r.w)
        for w in writes:
            need(w.w)
            for k, v in w.r.items():
                need((k, v))
        for ev in extra:
            need(ev)
        kn = self.know[eng]
        final = []
        for k, v in waits.items():
            if k == eng and not SELF_SYNC[eng]:
                continue
            if kn.get(k, 0) >= v:
                continue
            final.append((k, v))
        for k, v in final:
            if kn.get(k, 0) < v:
                kn[k] = v
            snap = self.evknow.get((k, v))
            if snap:
                for kk, vv in snap.items():
                    if kn.get(kk, 0) < vv:
                        kn[kk] = vv
        if dsem is not None:
            self.dcnt[dsem] += 16
            ev = (dsem, self.dcnt[dsem])
            self.evknow[ev] = dict(kn)
            inc = (dsem, 16)
        else:
            self.cnt[eng] += 1
            ev = (eng, self.cnt[eng])
            snap = dict(kn)
            snap[eng] = self.cnt[eng]
            self.evknow[ev] = snap
            inc = (eng, 1)
        self.q[eng].append((final, fn, inc))
        self.nwaits += len(final)
        self.nops += 1
        for r in reads:
            if r.r.get(ev[0], 0) < ev[1]:
                r.r[ev[0]] = ev[1]
        for w in writes:
            w.w = ev
            w.r = {}
        return ev

    def wait_only(self, eng, events):
        kn = self.know[eng]
        mx = {}
        for k, v in events:
            if mx.get(k, 0) < v:
                mx[k] = v
        final = [(k, v) for (k, v) in mx.items() if kn.get(k, 0) < v]
        for k, v in final:
            kn[k] = v
        self.q[eng].append((final, None, None))

    def emit(self):
        nc = self.nc
        sems = self.sems
        q = self.q

        def run(e, lst):
            for waits, fn, inc in lst:
                for k, v in waits:
                    e.wait_ge(sems[k], v)
                if fn is None:
                    continue
                ins = fn(e)
                ins.then_inc(sems[inc[0]], inc[1])

        with nc.Block() as block:
            @block.tensor
            def _(e):
                run(e, q["pe"])

            @block.scalar
            def _(e):
                run(e, q["act"])

            @block.vector
            def _(e):
                run(e, q["dve"])

            @block.gpsimd
            def _(e):
                run(e, q["pool"])

            @block.sync
            def _(e):
                run(e, q["sp"])
        self.stack.close()

from concourse.bass_utils import run_bass_kernel_spmd

D = 1024
S = 2048
NT = 4
DFF = 2816
DEPTH = 4
EPS = 1e-6
MLA_SCALE = 192.0 ** -0.5
FOX_SCALE = 128.0 ** -0.5
NEG = -30000.0
PI = float(np.pi)

W_A, W_B, W_C, W_SLOT, W_K, W_E = 16384, 8192, 8192, 4096, 672, 7472
ARENA = W_A + W_B + W_C + 3 * W_SLOT + W_K + W_E
assert ARENA <= 53200, ARENA


def MM(out, lhsT, rhs, start=True, stop=True):
    return lambda e: e.matmul(out, lhsT=lhsT, rhs=rhs, start=start, stop=stop)


def TR(out, in_, ident):
    return lambda e: e.transpose(out, in_, ident)


def GROUP(lst):
    def f(e):
        ins = None
        for g in lst:
            ins = g(e)
        return ins
    return f


def ACTV(out, in_, func, **kw):
    return lambda e: e.activation(out=out, in_=in_, func=func, **kw)


def STT(out, in0, scalar, in1, op0, op1):
    return lambda e: e.scalar_tensor_tensor(out=out, in0=in0, scalar=scalar, in1=in1, op0=op0, op1=op1)


def TS(out, in0, s1, s2=None, op0=ALU.mult, op1=None):
    if op1 is None:
        return lambda e: e.tensor_scalar(out=out, in0=in0, scalar1=s1, scalar2=None, op0=op0)
    return lambda e: e.tensor_scalar(out=out, in0=in0, scalar1=s1, scalar2=s2, op0=op0, op1=op1)


def TT(out, in0, in1, op):
    return lambda e: e.tensor_tensor(out=out, in0=in0, in1=in1, op=op)


def CP(out, in_):
    return lambda e: e.tensor_copy(out=out, in_=in_)


def DMA(out, in_):
    return lambda e: e.dma_start(out=out, in_=in_)


class Region:
    def __init__(self, arena, base, words):
        self.arena, self.base, self.words = arena, base, words
        self.fence = {}
        self.live = []
        self.off = 0

    def reset(self):
        f = fence_of(self.live)
        for k, v in f.items():
            if self.fence.get(k, 0) < v:
                self.fence[k] = v
        self.live = []
        self.off = 0

    def res(self, name):
        r = Res(name, self.fence)
        self.live.append(r)
        return r

    def raw(self, words, at=None):
        if at is None:
            at = self.off
            self.off += words
        assert at + words <= self.words, (at, words, self.words)
        return self.arena[:, self.base + at:self.base + at + words]

    def f32(self, words, at=None):
        return self.raw(words, at)

    def bf(self, elems, at=None):
        assert elems % 2 == 0
        return self.raw(elems // 2, at).bitcast(BF16)

    def i32(self, words, at=None):
        return self.raw(words, at).bitcast(I32)


class Rot:
    def __init__(self, items):
        self.items = list(items)
        self.i = 0

    def next(self):
        x = self.items[self.i % len(self.items)]
        self.i += 1
        return x


def build(layers, first, last):
    nc = bass.Bass("TRN2", target_bir_lowering=False)
    dr = {}

    def din(name, shape, dt=F32):
        dr[name] = nc.dram_tensor(name, list(shape), dt, kind="ExternalInput").ap()
        return dr[name]

    if first:
        x_d = din("x", [S, D])
    else:
        hin_d = din("hin", [128, 8, S])
    p_d = din("p", [DEPTH, S, 256])
    pos_d = din("pos", [1, S], I32)
    w_in_d = din("ffn_w_in", [DEPTH, 2, D, 2 * DFF])
    w_out_d = din("ffn_w_out", [DEPTH, 2, DFF, D])
    wdown_d = din("mla_w_down", [2, D, 448])
    wuq_d = din("mla_w_uq", [2, 256, 1536])
    wukv_d = din("mla_w_ukv", [2, 128, 2048])
    wo_mla_d = din("mla_w_o", [2, D, D])
    fox_in_d = din("fox_w_in", [2, D, 3080])
    wo_fox_d = din("fox_w_o", [2, D, D])
    wgate_d = din("ple_w_gate", [DEPTH, D, D])
    wproj_d = din("ple_w_proj", [DEPTH, 256, D])
    gtab_d = din("gtab", [142, 128])
    bft_d = din("bft", [8, 2])
    ident_d = din("ident", [128, 128])
    maskT_d = din("maskT", [128, 128])
    maskC_d = din("maskC", [128, 128])
    invf_d = din("invf", [64, 1])
    if last:
        out_d = nc.dram_tensor("out", [S, D], F32, kind="ExternalOutput").ap()
    else:
        hout_d = nc.dram_tensor("hout", [128, 8, S], F32, kind="ExternalOutput").ap()

    st = contextlib.ExitStack()
    with st:
        arena = st.enter_context(nc.sbuf_tensor("arena", [128, ARENA], F32))
        psum = st.enter_context(nc.psum_tensor("ps", [128, 8, 512], F32))
        P = Prog(nc)
        st.enter_context(P.stack)
        with contextlib.suppress(Exception):
            st.enter_context(nc.allow_low_precision("bf16 operands by design"))

        o = 0
        hT = arena[:, o:o + W_A].rearrange("p (c n) -> p c n", c=8); o += W_A
        RB = Region(arena, o, W_B); o += W_B
        RC = Region(arena, o, W_C); o += W_C
        slot_base = o; o += 3 * W_SLOT
        RK = Region(arena, o, W_K); o += W_K
        RE = Region(arena, o, W_E); o += W_E
        assert o == ARENA

        hres = [[Res("h%d_%d" % (c, t)) for t in range(NT)] for c in range(8)]
        bank = [psum[:, b, :] for b in range(8)]
        bres = [Res("bank%d" % b) for b in range(8)]
        slots = [arena[:, slot_base + i * W_SLOT: slot_base + (i + 1) * W_SLOT].bitcast(BF16) for i in range(3)]
        sres = [Res("slot%d" % i) for i in range(3)]
        ssem = [P.dma_sem("slot%d" % i) for i in range(3)]
        srot = Rot(range(3))
        misc_sem = P.dma_sem("misc")
        out_sems = [P.dma_sem("out0"), P.dma_sem("out1")]
        out_events = []

        def tcols(t):
            return slice(t * 512, (t + 1) * 512)

        ident = RK.f32(128)
        ones_bf = RK.bf(128)
        gcol = RK.f32(142)
        maskT = RK.f32(128)
        maskC = RK.f32(128)
        invf = RK.f32(2)
        bfcol = RK.f32(2)
        negb = RK.f32(2)
        wf = RK.bf(64).rearrange("p (k n) -> p k n", k=8)
        kres = RK.res("consts")
        wfres = RK.res("wf")
        wf_sem = P.dma_sem("wf")
        sq = [RE.bf(512), RE.bf(512)]
        sqres = [Res("sq0"), Res("sq1")]
        E_PH = RE.off
        RE2 = Region(arena, RE.base + E_PH, W_E - E_PH)

        evs = []
        evs.append(P.op("sp", DMA(ident, ident_d), writes=[kres], dsem=misc_sem))
        P.op("sp", DMA(maskT, maskT_d), writes=[kres], dsem=misc_sem)
        P.op("sp", DMA(maskC, maskC_d), writes=[kres], dsem=misc_sem)
        P.op("sp", DMA(invf[0:64, 0:1], invf_d), writes=[kres], dsem=misc_sem)
        P.op("sp", DMA(bfcol[0:8, 0:2], bft_d), writes=[kres], dsem=misc_sem)
        g1 = RC.f32(128)
        g2 = RC.f32(128)
        gres = RC.res("gstage")
        gt_sem = P.dma_sem("gtab")
        P.op("sp", DMA(g1, gtab_d[0:128, :]), writes=[gres], dsem=gt_sem)
        P.op("sp", DMA(g2[0:14, :], gtab_d[128:142, :]), writes=[gres], dsem=gt_sem)
        P.op("dve", lambda e: e.memset(ones_bf, 1.0), writes=[kres])
        P.op("dve", TS(negb[0:8, 0:2], bfcol[0:8, 0:2], -1.0), reads=[kres], writes=[kres])
        P.op("pe", TR(bank[0][:, 0:128], g1, ident), reads=[gres, kres], writes=[bres[0]])
        P.op("pe", TR(bank[1][:, 0:14], g2[0:14, :], ident[0:14, 0:14]), reads=[gres, kres], writes=[bres[1]])
        P.op("dve", CP(gcol[:, 0:128], bank[0][:, 0:128]), reads=[bres[0]], writes=[kres])
        P.op("dve", CP(gcol[:, 128:142], bank[1][:, 0:14]), reads=[bres[1]], writes=[kres])
        RC.reset()

        if first:
            NX = 6
            xst = [RC.f32(1024) for _ in range(NX)]
            xres = [RC.res("xst%d" % i) for i in range(NX)]
            xsem = [P.dma_sem("xst%d" % i) for i in range(NX)]
            brot = Rot([0, 1, 2, 3])
            for tt in range(16):
                b = tt % NX
                P.op("sp", DMA(xst[b], x_d[tt * 128:(tt + 1) * 128, :]), writes=[xres[b]], dsem=xsem[b])
                for half in range(2):
                    bk = brot.next()
                    for cc in range(4):
                        c = half * 4 + cc
                        P.op("pe", TR(bank[bk][:, cc * 128:(cc + 1) * 128], xst[b][:, c * 128:(c + 1) * 128], ident),
                             reads=[xres[b], kres], writes=[bres[bk]])
                    eng = "act" if half == 0 else "dve"
                    dst = hT[:, half * 4:half * 4 + 4, tt * 128:(tt + 1) * 128]
                    src = bank[bk].rearrange("p (c n) -> p c n", c=4)
                    wr = [hres[half * 4 + cc][tt // 4] for cc in range(4)]
                    if eng == "act":
                        P.op("act", ACTV(dst, src, AF.Copy), reads=[bres[bk]], writes=wr)
                    else:
                        P.op("dve", CP(dst, src), reads=[bres[bk]], writes=wr)
            RC.reset()
        else:
            for c in range(8):
                hsem = P.dma_sem("hin%d" % c)
                P.op("sp", DMA(hT[:, c, :], hin_d[:, c, :]), writes=hres[c], dsem=hsem)

        nrot = Rot([6, 7])

        def new_u():
            RB.reset()
            u = RB.bf(8 * S).rearrange("p (c n) -> p c n", c=8)
            ures = [[RB.res("u%d_%d" % (c, t)) for t in range(NT)] for c in range(8)]
            return u, ures

        def rstd_bank(t, bk, srcs, src_res, nfeat):
            n = len(srcs)
            for c in range(n):
                q = c % 2
                P.op("act", ACTV(sq[q], srcs[c], AF.Square), reads=[src_res[c]], writes=[sqres[q]])
                P.op("pe", MM(bank[bk], ones_bf, sq[q], start=(c == 0), stop=(c == n - 1)),
                     reads=[sqres[q], kres], writes=[bres[bk]])
            P.op("act", ACTV(bank[bk], bank[bk], AF.Ln, scale=1.0 / nfeat, bias=EPS), reads=[bres[bk]], writes=[bres[bk]])

        class NormJob:
            def __init__(self, gidx, old_ures=None):
                self.gidx = gidx
                self.old = old_ures
                RB.reset()
                self.u = RB.bf(8 * S).rearrange("p (c n) -> p c n", c=8)
                self.ures = [[None] * NT for _ in range(8)]
                self.nc_ = [0] * NT
                self.closed = [False] * NT
                self.bk = [None] * NT

            def step(self, t, c):
                assert self.nc_[t] == c
                if c == 0:
                    self.bk[t] = nrot.next()
                bk = self.bk[t]
                q = c % 2
                P.op("act", ACTV(sq[q], hT[:, c, tcols(t)], AF.Square), reads=[hres[c][t]], writes=[sqres[q]])
                P.op("pe", MM(bank[bk], ones_bf, sq[q], start=(c == 0), stop=(c == 7)),
                     reads=[sqres[q], kres], writes=[bres[bk]])
                self.nc_[t] = c + 1

            def close(self, t):
                assert self.nc_[t] == 8 and not self.closed[t]
                bk = self.bk[t]
                P.op("act", ACTV(bank[bk], bank[bk], AF.Ln, scale=1.0 / D, bias=EPS), reads=[bres[bk]], writes=[bres[bk]])
                P.op("act", ACTV(bank[bk], bank[bk], AF.Exp, scale=-0.5), reads=[bres[bk]], writes=[bres[bk]])
                for c in range(8):
                    f = dict(RB.fence)
                    if self.old is not None:
                        for k_, v_ in fence_of([self.old[c][t]]).items():
                            if f.get(k_, 0) < v_:
                                f[k_] = v_
                    r = Res("u%d_%d" % (c, t), f)
                    RB.live.append(r)
                    self.ures[c][t] = r
                    P.op("dve", STT(self.u[:, c, tcols(t)], hT[:, c, tcols(t)], gcol[:, self.gidx + c:self.gidx + c + 1], bank[bk], ALU.mult, ALU.mult),
                         reads=[hres[c][t], bres[bk], kres], writes=[r])
                self.closed[t] = True

            def finish(self):
                for t in range(NT):
                    if not self.closed[t]:
                        for c in range(self.nc_[t], 8):
                            self.step(t, c)
                        self.close(t)
                return self.u, self.ures

        def get_slot():
            i = srot.next()
            return i, slots[i], sres[i], ssem[i]

        def wdma(sr, sem, out, in_):
            return P.op("pool", DMA(out, in_), writes=[sr], dsem=sem)

        def proj_to_h(u_like, ures_like, nk, wslot, wres, scale, brot_, nj=None):
            for t in range(NT):
                for d in range(8):
                    bk = brot_.next()
                    P.op("pe", GROUP([MM(bank[bk], wslot[:, k, d * 128:(d + 1) * 128], u_like[:, k, tcols(t)], k == 0, k == nk - 1) for k in range(nk)]),
                         reads=[wres] + [ures_like[k][t] for k in range(nk)], writes=[bres[bk]])
                    P.op("dve", STT(hT[:, d, tcols(t)], bank[bk], scale, hT[:, d, tcols(t)], ALU.mult, ALU.add),
                         reads=[bres[bk], hres[d][t]], writes=[hres[d][t]])
                    if nj is not None and t >= 1:
                        nj.step(t - 1, d)
                        if d == 7:
                            nj.close(t - 1)

        def ffn(l, j, job, nxt_g):
            u, ures = job.finish()
            RC.reset()
            RE2.reset()
            g = RC.bf(8 * S).rearrange("p (c n) -> p c n", c=8)
            gres_ = [[RC.res("g%d_%d" % (c, t)) for t in range(NT)] for c in range(8)]
            stmp = [RE2.f32(512), RE2.f32(512)]
            stres = [RE2.res("st0"), RE2.res("st1")]
            strot = Rot([0, 1])
            prot = Rot([(0, 1), (2, 3)])
            orot = Rot([4, 5])
            nj = None
            win = w_in_d[l, j].rearrange("(k p) n -> p k n", p=128)
            wout = w_out_d[l, j].rearrange("(k p) n -> p k n", p=128)
            f0 = 0
            groups = ([4, 4], [4, 4], [4, 2])
            for gidx_, grp in enumerate(groups):
                gi = 0
                fk0 = f0
                for nb in grp:
                    si, sl, sr, sem = get_slot()
                    wg = sl[:, 0:8 * nb * 128].rearrange("p (k n) -> p k n", k=8)
                    wu = sl[:, 4096:4096 + 8 * nb * 128].rearrange("p (k n) -> p k n", k=8)
                    wdma(sr, sem, wg, win[:, :, f0 * 128:(f0 + nb) * 128])
                    wdma(sr, sem, wu, win[:, :, DFF + f0 * 128:DFF + (f0 + nb) * 128])
                    for f in range(nb):
                        for t in range(NT):
                            bg, bu = prot.next()
                            rd = [sr] + [ures[k][t] for k in range(8)]
                            P.op("pe", GROUP([MM(bank[bg], wg[:, k, f * 128:(f + 1) * 128], u[:, k, tcols(t)], k == 0, k == 7) for k in range(8)]),
                                 reads=rd, writes=[bres[bg]])
                            P.op("pe", GROUP([MM(bank[bu], wu[:, k, f * 128:(f + 1) * 128], u[:, k, tcols(t)], k == 0, k == 7) for k in range(8)]),
                                 reads=rd, writes=[bres[bu]])
                            q = strot.next()
                            P.op("act", ACTV(stmp[q], bank[bg], AF.Silu), reads=[bres[bg]], writes=[stres[q]])
                            P.op("dve", TT(g[:, gi, tcols(t)], stmp[q], bank[bu], ALU.mult),
                                 reads=[stres[q], bres[bu]], writes=[gres_[gi][t]])
                        gi += 1
                        f0 += 1
                nk = gi
                si, sl, sr, sem = get_slot()
                wo = sl[:, 0:nk * 1024].rearrange("p (k n) -> p k n", k=nk)
                wdma(sr, sem, wo, wout[:, fk0:fk0 + nk, :])
                if gidx_ == len(groups) - 1 and nxt_g is not None:
                    nj = NormJob(nxt_g, ures)
                proj_to_h(g, gres_, nk, wo, sr, 0.5, orot, nj)
            return nj

        def ple(l, job, nxt_g):
            u, ures = job.finish()
            RC.reset()
            RE2.reset()
            pT = RC.bf(2 * S).rearrange("p (c n) -> p c n", c=2)
            pTres = [[RC.res("pT%d_%d" % (c, t)) for t in range(NT)] for c in range(2)]
            pst = [RC.f32(1024).rearrange("p (a n) -> p a n", a=4) for _ in range(2)]
            pstres = [RC.res("pst0"), RC.res("pst1")]
            psem = [P.dma_sem("pst%d_%d" % (l, i)) for i in range(2)]
            sig = [RC.f32(512), RC.f32(512)]
            sigres = [RC.res("sig0"), RC.res("sig1")]
            t2 = [RC.f32(512), RC.f32(512)]
            t2res = [RC.res("t20"), RC.res("t21")]
            si, sl, srg, sem = get_slot()
            wg = sl.rearrange("p (k n) -> p k n", k=8)
            wdma(srg, sem, wg, wgate_d[l].rearrange("(k p) n -> p k n", p=128))
            si, sl2, srp, sem2 = get_slot()
            wp = sl2[:, 0:2048].rearrange("p (k n) -> p k n", k=2)
            wdma(srp, sem2, wp, wproj_d[l].rearrange("(k p) n -> p k n", p=128))
            brot_ = Rot([0, 1, 2, 3])
            for t in range(NT):
                b = t % 2
                P.op("sp", DMA(pst[b], p_d[l, t * 512:(t + 1) * 512, :].rearrange("(a p) n -> p a n", p=128)),
                     writes=[pstres[b]], dsem=psem[b])
                for kk in range(2):
                    bk = brot_.next()
                    for a in range(4):
                        P.op("pe", TR(bank[bk][:, a * 128:(a + 1) * 128], pst[b][:, a, kk * 128:(kk + 1) * 128], ident),
                             reads=[pstres[b], kres], writes=[bres[bk]])
                    P.op("act", ACTV(pT[:, kk, tcols(t)], bank[bk], AF.Copy), reads=[bres[bk]], writes=[pTres[kk][t]])
            grot = Rot([(0, 1), (2, 3), (4, 5)])
            r2 = Rot([0, 1])
            nj = NormJob(nxt_g, ures) if nxt_g is not None else None
            for t in range(NT):
                for d in range(8):
                    bg, bp = grot.next()
                    P.op("pe", GROUP([MM(bank[bg], wg[:, k, d * 128:(d + 1) * 128], u[:, k, tcols(t)], k == 0, k == 7) for k in range(8)]),
                         reads=[srg] + [ures[k][t] for k in range(8)], writes=[bres[bg]])
                    P.op("pe", GROUP([MM(bank[bp], wp[:, k, d * 128:(d + 1) * 128], pT[:, k, tcols(t)], k == 0, k == 1) for k in range(2)]),
                         reads=[srp, pTres[0][t], pTres[1][t]], writes=[bres[bp]])
                    q = r2.next()
                    P.op("act", ACTV(sig[q], bank[bg], AF.Sigmoid), reads=[bres[bg]], writes=[sigres[q]])
                    P.op("dve", TT(t2[q], sig[q], bank[bp], ALU.mult), reads=[sigres[q], bres[bp]], writes=[t2res[q]])
                    P.op("dve", TT(hT[:, d, tcols(t)], hT[:, d, tcols(t)], t2[q], ALU.add),
                         reads=[t2res[q], hres[d][t]], writes=[hres[d][t]])
                    if nj is not None and t >= 1:
                        nj.step(t - 1, d)
                        if d == 7:
                            nj.close(t - 1)
            return nj

        def attention_head(h, kind, qn, kn, V, qkv_res, oT, oTres, tmp, tmpres, PT, PTres, extra):
            srot_ = Rot([0, 1, 2, 3])
            trot = Rot(range(len(tmp)))
            prot_ = Rot(range(len(PT)))
            blocks = []
            for j in range(NT):
                for i in range(4 * (j + 1)):
                    blocks.append((j, i))
            state = {}
            scale = MLA_SCALE if kind == "mla" else FOX_SCALE
            mask = maskC if kind == "mla" else maskT

            def geom(j, i):
                off = i - 4 * j
                c0 = max(0, off) * 128
                return off, c0

            def emit_S(n):
                j, i = blocks[n]
                off, c0 = geom(j, i)
                bk = srot_.next()
                state[n] = {"sb": bk}
                qs = slice(j * 512 + c0, (j + 1) * 512)
                ks = slice(i * 128, (i + 1) * 128)
                if kind == "mla":
                    qr, kr = extra["qr"], extra["krope"]
                    grp = [MM(bank[bk][:, c0:512], kn[:, ks], qn[:, qs], True, False),
                           MM(bank[bk][:, c0:512], kr[:, ks], qr[:, qs], False, True)]
                else:
                    grp = [MM(bank[bk][:, c0:512], kn[:, ks], qn[:, qs], True, False),
                           MM(bank[bk][:, c0:512], extra["sel"], extra["cumref"][:, qs], False, True)]
                P.op("pe", GROUP(grp), reads=qkv_res + extra.get("sres", []), writes=[bres[bk]])

            def emit_E(n):
                j, i = blocks[n]
                off, c0 = geom(j, i)
                bk = state[n]["sb"]
                pi = prot_.next()
                state[n]["pt"] = pi
                kw = {}
                rd = []
                if kind == "fox":
                    kw = dict(bias=extra["negcumK"][:, i * 8 + h:i * 8 + h + 1])
                    rd = extra["ncres"]
                if off < 0:
                    P.op("act", ACTV(PT[pi], bank[bk], AF.Exp, scale=scale, **kw), reads=[bres[bk]] + rd, writes=[PTres[pi]])
                else:
                    ti = trot.next()
                    P.op("dve", STT(tmp[ti][:, 0:128], bank[bk][:, c0:c0 + 128], scale, mask, ALU.mult, ALU.add),
                         reads=[bres[bk], kres], writes=[tmpres[ti]])
                    P.op("act", ACTV(PT[pi][:, c0:c0 + 128], tmp[ti][:, 0:128], AF.Exp, **kw), reads=[tmpres[ti]] + rd, writes=[PTres[pi]])
                    if c0 + 128 < 512:
                        P.op("act", ACTV(PT[pi][:, c0 + 128:512], bank[bk][:, c0 + 128:512], AF.Exp, scale=scale, **kw),
                             reads=[bres[bk]] + rd, writes=[PTres[pi]])

            pending = []

            def emit_PV(n):
                j, i = blocks[n]
                off, c0 = geom(j, i)
                pi = state[n]["pt"]
                ob, sb_ = 4 + (j % 2), 6 + (j % 2)
                last_i = 4 * (j + 1) - 1
                P.op("pe", GROUP([MM(bank[ob][:, c0:512], V[:, i, :], PT[pi][:, c0:512], i == 0, i == last_i),
                                  MM(bank[sb_][:, c0:512], ones_bf, PT[pi][:, c0:512], i == 0, i == last_i)]),
                     reads=[PTres[pi], kres] + qkv_res, writes=[bres[ob], bres[sb_]])
                if i == last_i:
                    pending.append((n + 2, j, ob, sb_))
                del state[n]

            def emit_final(j, ob, sb_):
                ti = trot.next()
                P.op("act", ACTV(tmp[ti], bank[sb_], AF.Ln), reads=[bres[sb_]], writes=[tmpres[ti]])
                P.op("act", ACTV(tmp[ti], tmp[ti], AF.Exp, scale=-1.0), reads=[tmpres[ti]], writes=[tmpres[ti]])
                P.op("dve", TT(oT[:, h, tcols(j)], bank[ob], tmp[ti], ALU.mult),
                     reads=[bres[ob], tmpres[ti]], writes=[oTres[h][j]])

            nb = len(blocks)
            LA = 3
            for n in range(min(LA, nb)):
                emit_S(n)
            emit_E(0)
            for n in range(nb):
                if n + LA < nb:
                    emit_S(n + LA)
                if n + 1 < nb:
                    emit_E(n + 1)
                emit_PV(n)
                while pending and pending[0][0] <= n:
                    _, j_, ob_, sb__ = pending.pop(0)
                    emit_final(j_, ob_, sb__)
            while pending:
                _, j_, ob_, sb__ = pending.pop(0)
                emit_final(j_, ob_, sb__)

        def mla(l, jm, job, nxt_g):
            u, ures = job.finish()
            RC.reset()
            RE2.reset()
            cosT = RE2.bf(S)
            sinT = RE2.bf(S)
            cqn = RE2.bf(2 * S).rearrange("p (c n) -> p c n", c=2)
            ckvn = RE2.bf(S)
            krope = RE2.bf(S)
            tabres = RE2.res("tab")
            cqres = [[RE2.res("cq%d_%d" % (c, t)) for t in range(NT)] for c in range(2)]
            ckres = [RE2.res("ck%d" % t) for t in range(NT)]
            krres = [RE2.res("kr%d" % t) for t in range(NT)]
            P.op("dve", lambda e: e.memset(krope[64:128, :], 0.0), writes=krres)
            posi = RC.i32(2048)
            ang = RC.f32(2048)
            ki = RC.i32(2048)
            kf = RC.f32(2048)
            tr = RC.res("trig")
            psem_ = P.dma_sem("pos%d" % l)
            P.op("sp", DMA(posi[0:64, :], pos_d.partition_broadcast(64)), writes=[tr], dsem=psem_)
            C1 = 6.28125
            C2 = float(np.float32(2 * np.pi - 6.28125))
            A64 = slice(0, 64)
            P.op("dve", CP(ang[A64, :], posi[A64, :]), reads=[tr], writes=[tr])
            P.op("dve", TS(ang[A64, :], ang[A64, :], invf[0:64, 0:1]), reads=[tr, kres], writes=[tr])
            P.op("dve", TS(ki[A64, :], ang[A64, :], float(1.0 / (2 * np.pi))), reads=[tr], writes=[tr])
            P.op("dve", CP(kf[A64, :], ki[A64, :]), reads=[tr], writes=[tr])
            P.op("dve", STT(ang[A64, :], kf[A64, :], -C1, ang[A64, :], ALU.mult, ALU.add), reads=[tr], writes=[tr])
            P.op("dve", STT(ang[A64, :], kf[A64, :], -C2, ang[A64, :], ALU.mult, ALU.add), reads=[tr], writes=[tr])
            P.op("dve", TS(ang[A64, :], ang[A64, :], -3.1415925, 3.1415925, ALU.max, ALU.min), reads=[tr], writes=[tr])
            P.op("act", ACTV(sinT[A64, :], ang[A64, :], AF.Sin), reads=[tr], writes=[tabres])
            P.op("dve", STT(kf[A64, :], ang[A64, :], -1.0, ang[A64, :], ALU.mult, ALU.max), reads=[tr], writes=[tr])
            P.op("dve", TS(kf[A64, :], kf[A64, :], -1.0, PI / 2, ALU.mult, ALU.add), reads=[tr], writes=[tr])
            P.op("act", ACTV(cosT[A64, :], kf[A64, :], AF.Sin), reads=[tr], writes=[tabres])
            RC.reset()
            ia, sa, sra, sema = get_slot()
            wd = sa.rearrange("p (k n) -> p k n", k=8)
            wdma(sra, sema, wd[:, :, 0:448], wdown_d[jm].rearrange("(k p) n -> p k n", p=128))
            P.op("dve", TS(wd[:, :, 448:480], wd[:, :, 416:448], -1.0), reads=[sra], writes=[sra])
            P.op("dve", CP(wd[:, :, 480:512], wd[:, :, 384:416]), reads=[sra], writes=[sra])
            ib, sb, srb, semb = get_slot()
            wuq = sb[:, 0:3072].rearrange("p (k n) -> p k n", k=2)
            wukv = sb[:, 3072:5120]
            wuqr = sb[:, 5120:6144].rearrange("p (k n) -> p k n", k=2)
            wdma(srb, semb, wuq, wuq_d[jm].rearrange("(k p) n -> p k n", p=128))
            wdma(srb, semb, wukv, wukv_d[jm])
            wuq4 = wuq.rearrange("p k (h m) -> p k h m", h=8)
            wuqr4 = wuqr.rearrange("p k (h m) -> p k h m", h=8)
            P.op("dve", TS(wuqr4[:, :, :, 0:32], wuq4[:, :, :, 160:192], -1.0), reads=[srb], writes=[srb])
            P.op("dve", CP(wuqr4[:, :, :, 32:64], wuq4[:, :, :, 128:160]), reads=[srb], writes=[srb])
            sqd = [RC.bf(512) for _ in range(3)]
            sqdres = [RC.res("sqd%d" % i) for i in range(3)]
            rsb = [RC.f32(512), RC.f32(512)]
            rsres = [RC.res("rs0"), RC.res("rs1")]
            rt = [RC.f32(512), RC.f32(512)]
            rtres = [RC.res("rt0"), RC.res("rt1")]
            gq = 136 + jm * 2
            gkv = 140 + jm
            for t in range(NT):
                rd = [sra] + [ures[k][t] for k in range(8)]
                for bi, (c0_, c1_) in enumerate([(0, 128), (128, 256), (256, 384), (384, 448), (448, 512)]):
                    m = c1_ - c0_
                    P.op("pe", GROUP([MM(bank[bi][0:m, :], wd[:, k, c0_:c1_], u[:, k, tcols(t)], k == 0, k == 7) for k in range(8)]),
                         reads=rd, writes=[bres[bi]])
                for bi in range(3):
                    P.op("act", ACTV(sqd[bi], bank[bi], AF.Square), reads=[bres[bi]], writes=[sqdres[bi]])
                P.op("pe", MM(bank[5], ones_bf, sqd[0], True, False), reads=[sqdres[0], kres], writes=[bres[5]])
                P.op("pe", MM(bank[5], ones_bf, sqd[1], False, True), reads=[sqdres[1], kres], writes=[bres[5]])
                P.op("pe", MM(bank[6], ones_bf, sqd[2], True, True), reads=[sqdres[2], kres], writes=[bres[6]])
                P.op("act", ACTV(bank[5], bank[5], AF.Ln, scale=1.0 / 256, bias=EPS), reads=[bres[5]], writes=[bres[5]])
                P.op("act", ACTV(rsb[0], bank[5], AF.Exp, scale=-0.5), reads=[bres[5]], writes=[rsres[0]])
                P.op("act", ACTV(bank[6], bank[6], AF.Ln, scale=1.0 / 128, bias=EPS), reads=[bres[6]], writes=[bres[6]])
                P.op("act", ACTV(rsb[1], bank[6], AF.Exp, scale=-0.5), reads=[bres[6]], writes=[rsres[1]])
                for i in range(2):
                    P.op("dve", STT(cqn[:, i, tcols(t)], bank[i], gcol[:, gq + i:gq + i + 1], rsb[0], ALU.mult, ALU.mult),
                         reads=[bres[i], rsres[0], kres], writes=[cqres[i][t]])
                P.op("dve", STT(ckvn[:, tcols(t)], bank[2], gcol[:, gkv:gkv + 1], rsb[1], ALU.mult, ALU.mult),
                     reads=[bres[2], rsres[1], kres], writes=[ckres[t]])
                P.op("dve", TT(rt[0][A64, :], bank[3][A64, :], cosT[A64, tcols(t)], ALU.mult), reads=[bres[3], tabres], writes=[rtres[0]])
                P.op("dve", TT(rt[1][A64, :], bank[4][A64, :], sinT[A64, tcols(t)], ALU.mult), reads=[bres[4], tabres], writes=[rtres[1]])
                P.op("dve", TT(krope[A64, tcols(t)], rt[0][A64, :], rt[1][A64, :], ALU.add), reads=[rtres[0], rtres[1]], writes=[krres[t]])
            RC.reset()
            RB.reset()
            oT = RC.bf(8 * S).rearrange("p (c n) -> p c n", c=8)
            oTres = [[RC.res("o%d_%d" % (c, t)) for t in range(NT)] for c in range(8)]
            qn = RB.bf(S)
            qr = RB.bf(S)
            kn = RB.bf(S)
            V = RB.bf(S).rearrange("p (a n) -> p a n", a=16)
            hres_ = RB.res("headbufs")
            rt = [RB.f32(512), RB.f32(512)]
            rtres = [RB.res("rt0"), RB.res("rt1")]
            tmp = [RB.f32(512) for _ in range(3)]
            tmpres = [RB.res("tmp%d" % i) for i in range(3)]
            PT = [RB.bf(512) for _ in range(3)]
            PTres = [RB.res("PT%d" % i) for i in range(3)]
            prj = Rot([0, 1, 2, 3])
            P.op("dve", lambda e: e.memset(qr[64:128, :], 0.0), writes=[hres_])
            for h in range(8):
                for t in range(NT):
                    bk = prj.next()
                    P.op("pe", GROUP([MM(bank[bk], wuq[:, i, h * 192:h * 192 + 128], cqn[:, i, tcols(t)], i == 0, i == 1) for i in range(2)]),
                         reads=[srb, cqres[0][t], cqres[1][t]], writes=[bres[bk]])
                    P.op("act", ACTV(qn[:, tcols(t)], bank[bk], AF.Copy), reads=[bres[bk]], writes=[hres_])
                    ba = prj.next()
                    P.op("pe", GROUP([MM(bank[ba][A64, :], wuq[:, i, h * 192 + 128:h * 192 + 192], cqn[:, i, tcols(t)], i == 0, i == 1) for i in range(2)]),
                         reads=[srb, cqres[0][t], cqres[1][t]], writes=[bres[ba]])
                    bb = prj.next()
                    P.op("pe", GROUP([MM(bank[bb][A64, :], wuqr[:, i, h * 64:(h + 1) * 64], cqn[:, i, tcols(t)], i == 0, i == 1) for i in range(2)]),
                         reads=[srb, cqres[0][t], cqres[1][t]], writes=[bres[bb]])
                    P.op("dve", TT(rt[0][A64, :], bank[ba][A64, :], cosT[A64, tcols(t)], ALU.mult), reads=[bres[ba], tabres], writes=[rtres[0]])
                    P.op("dve", TT(rt[1][A64, :], bank[bb][A64, :], sinT[A64, tcols(t)], ALU.mult), reads=[bres[bb], tabres], writes=[rtres[1]])
                    P.op("dve", TT(qr[A64, tcols(t)], rt[0][A64, :], rt[1][A64, :], ALU.add), reads=[rtres[0], rtres[1]], writes=[hres_])
                    bk = prj.next()
                    P.op("pe", MM(bank[bk], wukv[:, h * 256:h * 256 + 128], ckvn[:, tcols(t)], True, True),
                         reads=[srb, ckres[t]], writes=[bres[bk]])
                    P.op("act", ACTV(kn[:, tcols(t)], bank[bk], AF.Copy), reads=[bres[bk]], writes=[hres_])
                    bk = prj.next()
                    P.op("pe", GROUP([MM(bank[bk][:, a * 128:(a + 1) * 128], ckvn[:, t * 512 + a * 128:t * 512 + (a + 1) * 128],
                                         wukv[:, h * 256 + 128:h * 256 + 256], True, True) for a in range(4)]),
                         reads=[srb, ckres[t]], writes=[bres[bk]])
                    P.op("act", ACTV(V[:, t * 4:(t + 1) * 4, :], bank[bk].rearrange("p (a n) -> p a n", a=4), AF.Copy),
                         reads=[bres[bk]], writes=[hres_])
                attention_head(h, "mla", qn, kn, V, [hres_] + krres, oT, oTres, tmp, tmpres, PT, PTres,
                               {"qr": qr, "krope": krope})
            ic, sc, src_, semc = get_slot()
            wo = sc.rearrange("p (k n) -> p k n", k=8)
            wdma(src_, semc, wo, wo_mla_d[jm].rearrange("(k p) n -> p k n", p=128))
            nj = NormJob(nxt_g, None) if nxt_g is not None else None
            proj_to_h(oT, oTres, 8, wo, src_, 1.0, Rot([0, 1, 2, 3]), nj)
            return nj

        def fox(l, jf, job, nxt_g):
            u, ures = job.finish()
            RC.reset()
            RE2.reset()
            oT = RC.bf(8 * S).rearrange("p (c n) -> p c n", c=8)
            oTres = [[RC.res("o%d_%d" % (c, t)) for t in range(NT)] for c in range(8)]
            spb = RE2.f32(2048, at=4096)
            cumneg = RE2.f32(2048, at=0)
            fr1 = RE2.res("spb")
            fr = RE2.res("cumneg")
            win = fox_in_d[jf].rearrange("(k p) n -> p k n", p=128)
            P.op("pool", DMA(wf, win[:, :, 3072:3080]), writes=[wfres], dsem=wf_sem)
            A8 = slice(0, 8)
            prj = Rot([0, 1, 2, 3])
            for t in range(NT):
                bk = prj.next()
                P.op("pe", GROUP([MM(bank[bk][A8, :], wf[:, k, :], u[:, k, tcols(t)], k == 0, k == 7) for k in range(8)]),
                     reads=[wfres] + [ures[k][t] for k in range(8)], writes=[bres[bk]])
                P.op("act", ACTV(spb[A8, tcols(t)], bank[bk][A8, :], AF.Exp, scale=-1.0, bias=negb[0:8, jf:jf + 1]),
                     reads=[bres[bk], kres], writes=[fr1])
            P.op("act", ACTV(spb[A8, :], spb[A8, :], AF.Ln, bias=1.0), reads=[fr1], writes=[fr1])
            P.op("dve", lambda e: e.tensor_tensor_scan(out=cumneg[A8, :], data0=spb[A8, :], data1=spb[A8, :], initial=0.0, op0=ALU.add, op1=ALU.max),
                 reads=[fr1], writes=[fr])
            for k_, v_ in fence_of([fr1]).items():
                if RE2.fence.get(k_, 0) < v_:
                    RE2.fence[k_] = v_
            RE2.off = 3072
            cumref = RE2.bf(S)
            negcumK = RE2.f32(128)
            sel = RE2.bf(128)
            ncres = RE2.res("negcum")
            selres = RE2.res("sel")
            tmp = [RE2.f32(512) for _ in range(3)]
            tmpres = [RE2.res("tmp%d" % i) for i in range(3)]
            PT = [RE2.bf(512) for _ in range(4)]
            PTres = [RE2.res("PT%d" % i) for i in range(4)]
            bk = prj.next()
            P.op("pe", GROUP([TR(bank[bk][:, a * 8:(a + 1) * 8], cumneg[A8, a * 128:(a + 1) * 128], ident[0:8, 0:8]) for a in range(16)]),
                 reads=[fr, kres], writes=[bres[bk]])
            P.op("dve", CP(negcumK, bank[bk][:, 0:128]), reads=[bres[bk]], writes=[ncres])
            P.op("dve", lambda e: e.memset(cumref, 0.0), writes=[ncres])
            P.op("dve", TS(cumref[A8, :], cumneg[A8, :], -1.0 / FOX_SCALE), reads=[fr, ncres], writes=[ncres])
            hfence = fence_of([fr])
            qn = RE2.bf(S, at=0)
            kn = RE2.bf(S, at=1024)
            V = RE2.bf(S, at=2048).rearrange("p (a n) -> p a n", a=16)
            hres_ = Res("headbufs", {**RE2.fence, **{k: max(v, RE2.fence.get(k, 0)) for k, v in hfence.items()}})
            RE2.live.append(hres_)
            for h in range(8):
                ih, sh, srh, semh = get_slot()
                wh = sh[:, 0:8 * 384].rearrange("p (k n) -> p k n", k=8)
                for part in range(3):
                    wdma(srh, semh, wh[:, :, part * 128:(part + 1) * 128], win[:, :, part * 1024 + h * 128:part * 1024 + (h + 1) * 128])
                for t in range(NT):
                    rd = [srh] + [ures[k][t] for k in range(8)]
                    bk = prj.next()
                    P.op("pe", GROUP([MM(bank[bk], wh[:, k, 0:128], u[:, k, tcols(t)], k == 0, k == 7) for k in range(8)]), reads=rd, writes=[bres[bk]])
                    P.op("act", ACTV(qn[:, tcols(t)], bank[bk], AF.Copy), reads=[bres[bk]], writes=[hres_])
                    bk = prj.next()
                    P.op("pe", GROUP([MM(bank[bk], wh[:, k, 128:256], u[:, k, tcols(t)], k == 0, k == 7) for k in range(8)]), reads=rd, writes=[bres[bk]])
                    P.op("act", ACTV(kn[:, tcols(t)], bank[bk], AF.Copy), reads=[bres[bk]], writes=[hres_])
                    bk = prj.next()
                    grp = []
                    for a in range(4):
                        for k in range(8):
                            grp.append(MM(bank[bk][:, a * 128:(a + 1) * 128], u[:, k, t * 512 + a * 128:t * 512 + (a + 1) * 128], wh[:, k, 256:384], k == 0, k == 7))
                    P.op("pe", GROUP(grp), reads=rd, writes=[bres[bk]])
                    P.op("dve", CP(V[:, t * 4:(t + 1) * 4, :], bank[bk].rearrange("p (a n) -> p a n", a=4)), reads=[bres[bk]], writes=[hres_])
                P.op("dve", TS(sel, ones_bf, ident[:, h:h + 1]), reads=[kres], writes=[selres])
                attention_head(h, "fox", qn, kn, V, [hres_], oT, oTres, tmp, tmpres, PT, PTres,
                               {"sel": sel, "cumref": cumref, "negcumK": negcumK, "ncres": [ncres], "sres": [selres, ncres]})
            ic, sc, src_, semc = get_slot()
            wo = sc.rearrange("p (k n) -> p k n", k=8)
            wdma(src_, semc, wo, wo_fox_d[jf].rearrange("(k p) n -> p k n", p=128))
            nj = NormJob(nxt_g, ures) if nxt_g is not None else None
            proj_to_h(oT, oTres, 8, wo, src_, 1.0, Rot([0, 1, 2, 3]), nj)
            return nj

        phases = []
        for l in layers:
            phases += [("ffn", l, 0), ("mix", l, 0), ("ffn", l, 1), ("ple", l, 0)]

        def gidx_of(ph):
            kind_, l_, j_ = ph
            if kind_ == "ffn":
                return (l_ * 2 + j_) * 8
            if kind_ == "mix":
                return 64 + l_ * 8
            return 96 + l_ * 8

        job = NormJob(gidx_of(phases[0]), None) if phases else None
        for pi_, ph in enumerate(phases):
            nxt_g = gidx_of(phases[pi_ + 1]) if pi_ + 1 < len(phases) else None
            kind_, l_, j_ = ph
            if kind_ == "ffn":
                job = ffn(l_, j_, job, nxt_g)
            elif kind_ == "mix":
                job = mla(l_, l_ // 2, job, nxt_g) if l_ % 2 == 0 else fox(l_, l_ // 2, job, nxt_g)
            else:
                job = ple(l_, job, nxt_g)

        if last:
            RC.reset()
            RB.reset()
            yT = [RC.f32(512) for _ in range(8)]
            yres = [RC.res("y%d" % c) for c in range(8)]
            yst = [RC.f32(1024), RC.f32(1024)]
            ysres = [RC.res("ys0"), RC.res("ys1")]
            brot_ = Rot([0, 1, 2, 3])
            n_out = 0
            for t in range(NT):
                bk = nrot.next()
                rstd_bank(t, bk, [hT[:, c, tcols(t)] for c in range(8)], [hres[c][t] for c in range(8)], D)
                P.op("act", ACTV(bank[bk], bank[bk], AF.Exp, scale=-0.5), reads=[bres[bk]], writes=[bres[bk]])
                for c in range(8):
                    P.op("dve", STT(yT[c], hT[:, c, tcols(t)], gcol[:, 128 + c:129 + c], bank[bk], ALU.mult, ALU.mult),
                         reads=[hres[c][t], bres[bk], kres], writes=[yres[c]])
                for a in range(4):
                    b = n_out % 2
                    n_out += 1
                    for half in range(2):
                        tb = brot_.next()
                        for cc in range(4):
                            c = half * 4 + cc
                            P.op("pe", TR(bank[tb][:, cc * 128:(cc + 1) * 128], yT[c][:, a * 128:(a + 1) * 128], ident),
                                 reads=[yres[c], kres], writes=[bres[tb]])
                        if half == 0:
                            P.op("act", ACTV(yst[b][:, 0:512], bank[tb], AF.Copy), reads=[bres[tb]], writes=[ysres[b]])
                        else:
                            P.op("dve", CP(yst[b][:, 512:1024], bank[tb]), reads=[bres[tb]], writes=[ysres[b]])
                    r0 = t * 512 + a * 128
                    out_events.append(P.op("sp", DMA(out_d[r0:r0 + 128, :], yst[b]), reads=[ysres[b]], dsem=out_sems[b]))
        else:
            for c in range(8):
                out_events.append(P.op("sp", DMA(hout_d[:, c, :], hT[:, c, :]), reads=hres[c], dsem=out_sems[0]))
        P.wait_only("sp", out_events[-2:])
        P.emit()
    return nc, P


_CACHE = {}


def _get_prog(layers, first, last):
    key = (tuple(layers), first, last)
    if key not in _CACHE:
        _CACHE[key] = build(list(layers), first, last)[0]
    return _CACHE[key]


LAUNCH_PLAN = [([0, 1, 2, 3], True, True)]


def kernel(x, p, positions, ffn_norm, ffn_w_in, ffn_w_out, mix_norm,
           mla_w_down, mla_q_norm, mla_w_uq, mla_kv_norm, mla_w_ukv, mla_w_o,
           fox_w_in, fox_b_f, fox_w_o, ple_norm, ple_w_gate, ple_w_proj, final_norm):
    f32 = np.float32
    A = lambda a: np.ascontiguousarray(np.asarray(a))
    x = A(x); p = A(p); positions = A(positions)
    B = x.shape[0]
    gtab = np.concatenate([
        A(ffn_norm).reshape(-1, 128), A(mix_norm).reshape(-1, 128), A(ple_norm).reshape(-1, 128),
        A(final_norm).reshape(-1, 128), A(mla_q_norm).reshape(-1, 128), A(mla_kv_norm).reshape(-1, 128)], axis=0).astype(f32)
    assert gtab.shape == (142, 128)
    bft = A(A(fox_b_f).T).astype(f32)
    kk = np.arange(128)
    ident = np.eye(128, dtype=f32)
    maskT = np.where(kk[:, None] <= kk[None, :], 0.0, NEG).astype(f32)
    maskC = np.where((kk[:, None] // 64) <= (kk[None, :] // 64), 0.0, NEG).astype(f32)
    inv = (np.float32(10000.0) ** (-np.arange(0, 64, 2, dtype=f32) / np.float32(64))).astype(f32)
    invf = np.concatenate([inv, inv])[:, None].astype(f32)
    shared = {
        "ffn_w_in": A(ffn_w_in), "ffn_w_out": A(ffn_w_out), "mla_w_down": A(mla_w_down), "mla_w_uq": A(mla_w_uq),
        "mla_w_ukv": A(mla_w_ukv), "mla_w_o": A(mla_w_o), "fox_w_in": A(fox_w_in), "fox_w_o": A(fox_w_o),
        "ple_w_gate": A(ple_w_gate), "ple_w_proj": A(ple_w_proj), "gtab": gtab, "bft": bft, "ident": ident,
        "maskT": maskT, "maskC": maskC, "invf": invf,
    }
    h = None
    outp = None
    for layers, first, last in LAUNCH_PLAN:
        nc = _get_prog(layers, first, last)
        in_maps = []
        for b in range(B):
            m = dict(shared)
            m["p"] = A(p[:, b])
            m["pos"] = A(positions[b:b + 1]).astype(np.int32)
            if first:
                m["x"] = x[b]
            else:
                m["hin"] = h[b]
            in_maps.append(m)
        res = run_bass_kernel_spmd(nc, in_maps, core_ids=list(range(B)))
        if last:
            outp = np.stack([np.asarray(r["out"]) for r in res.results], axis=0).astype(f32)
        else:
            h = [np.asarray(r["hout"]) for r in res.results]
    return outp
```
